# Optimizing a Trainium2 kernel written in Bass

```python
import math
import jax, jax.numpy as jnp
from jax import lax
import numpy as np

D_MODEL = 4096
BATCH = 1
SEQ = 8192
DEPTH = 1
DEC_BATCH = 8
DEC_SEQ = 64
PAST_LEN = 2048

CHUNK = 64
Q_BLOCK = 128
ROPE_THETA = 500000.0
EPS = 1e-6
NEG = -1e30
H_A = 16
DH_A = 64
DV_A = 2 * DH_A
ROT_A = DH_A // 4
SCALE_A = DH_A ** -0.5
H_B = 16
Q_LORA = 896
KV_LORA = 512
D_NOPE = 128
D_ROPE = 64
DV_B = 128
SCALE_B = (D_NOPE + D_ROPE) ** -0.5
D_FF = 11008
CONV_W = 3
N_QK_A = 2 * H_A * DH_A
N_V_A = H_A * DV_A
N_KV_B = KV_LORA + D_ROPE
N_GATE = 2 * D_MODEL
SPLITS = [N_QK_A, 2 * N_QK_A, 2 * N_QK_A + N_V_A, 2 * N_QK_A + N_V_A + Q_LORA,
          2 * N_QK_A + N_V_A + Q_LORA + N_KV_B]
N_IN = SPLITS[-1] + N_GATE

kernel_name = 'gated_diffattn_mla_convffn_stream_step'


def _rmsnorm(x, g):
    xf = x.astype(jnp.float32)
    y = xf * lax.rsqrt(jnp.mean(xf * xf, axis=-1, keepdims=True) + EPS)
    return (y * g.astype(jnp.float32)).astype(x.dtype)


def _rope(x, pos, rot):
    half = rot // 2
    inv = jnp.power(jnp.float32(ROPE_THETA), -jnp.arange(half, dtype=jnp.float32) / half)
    ang = pos.astype(jnp.float32)[:, None] * inv[None, :]
    cos = jnp.cos(ang)[None, :, None, :]
    sin = jnp.sin(ang)[None, :, None, :]
    xr = x[..., :rot].astype(jnp.float32)
    x1, x2 = xr[..., :half], xr[..., half:]
    xr = jnp.concatenate([x1 * cos - x2 * sin, x2 * cos + x1 * sin], axis=-1).astype(x.dtype)
    return jnp.concatenate([xr, x[..., rot:]], axis=-1)


def _diff_attend(q1, q2, k1, k2, v, lam, mask):
    s1 = jnp.einsum('bqhd,bkhd->bhqk', q1, k1).astype(jnp.float32) * SCALE_A
    s2 = jnp.einsum('bqhd,bkhd->bhqk', q2, k2).astype(jnp.float32) * SCALE_A
    if mask is not None:
        s1 = jnp.where(mask, s1, NEG)
        s2 = jnp.where(mask, s2, NEG)
    p = jax.nn.softmax(s1, axis=-1) - lam * jax.nn.softmax(s2, axis=-1)
    return jnp.einsum('bhqk,bkhd->bqhd', p.astype(v.dtype), v)


def _mla_attend(q_nope, q_pe, k_nope, k_pe, v, mask):
    s = (jnp.einsum('bqhd,bkhd->bhqk', q_nope, k_nope)
         + jnp.einsum('bqhd,bkd->bhqk', q_pe, k_pe)).astype(jnp.float32) * SCALE_B
    if mask is not None:
        s = jnp.where(mask, s, NEG)
    p = jax.nn.softmax(s, axis=-1)
    return jnp.einsum('bhqk,bkhd->bqhd', p.astype(v.dtype), v)


def _sweep(fn, qs, T):
    nblk = T // Q_BLOCK
    qb = tuple(jnp.swapaxes(q.reshape(q.shape[0], nblk, Q_BLOCK, *q.shape[2:]), 0, 1) for q in qs)
    kchunk = jnp.arange(T) // CHUNK

    def body(args):
        i, blk = args
        qchunk = (i * Q_BLOCK + jnp.arange(Q_BLOCK)) // CHUNK
        mask = kchunk[None, :] <= qchunk[:, None]
        return fn(*blk, mask)

    out = lax.map(body, (jnp.arange(nblk), qb))
    out = jnp.swapaxes(out, 0, 1)
    return out.reshape(out.shape[0], T, *out.shape[3:])


def _layer(x, pos, past, lam_init, p):
    B, T, _ = x.shape
    xn = _rmsnorm(x, p['g_attn'])
    proj = xn @ p['w_in']
    q_a, k_a, v_a, c_q, c_kv, gate = jnp.split(proj, SPLITS, axis=-1)
    qa = _rope(q_a.reshape(B, T, 2 * H_A, DH_A), pos, ROT_A)
    ka = _rope(k_a.reshape(B, T, 2 * H_A, DH_A), pos, ROT_A)
    va = v_a.reshape(B, T, H_A, DV_A)
    qb = (_rmsnorm(c_q, p['g_qa']) @ p['w_qb']).reshape(B, T, H_B, D_NOPE + D_ROPE)
    qb_nope = qb[..., :D_NOPE]
    qb_pe = _rope(qb[..., D_NOPE:], pos, D_ROPE)
    ckv = _rmsnorm(c_kv[..., :KV_LORA], p['g_kva'])
    kpe = _rope(c_kv[..., None, KV_LORA:], pos, D_ROPE)[:, :, 0]
    if past is None:
        ka_all, va_all, ckv_all, kpe_all = ka, va, ckv, kpe
        conv_past = jnp.zeros((B, CONV_W - 1, D_FF), x.dtype)
    else:
        ka_all = jnp.concatenate([past['dk'], ka], axis=1)
        va_all = jnp.concatenate([past['dv'], va], axis=1)
        ckv_all = jnp.concatenate([past['ckv'], ckv], axis=1)
        kpe_all = jnp.concatenate([past['kpe'], kpe], axis=1)
        conv_past = past['conv']
    kvb = (ckv_all @ p['w_kvb']).reshape(B, -1, H_B, D_NOPE + DV_B)
    kb_nope, vb = kvb[..., :D_NOPE], kvb[..., D_NOPE:]
    f32 = jnp.float32
    lam = (jnp.exp(jnp.sum(p['lam_q1'].astype(f32) * p['lam_k1'].astype(f32)))
           - jnp.exp(jnp.sum(p['lam_q2'].astype(f32) * p['lam_k2'].astype(f32))) + lam_init)
    k1, k2 = ka_all[:, :, :H_A], ka_all[:, :, H_A:]
    q1, q2 = qa[:, :, :H_A], qa[:, :, H_A:]
    fa = lambda a1, a2, mask: _diff_attend(a1, a2, k1, k2, va_all, lam, mask)
    fb = lambda bn, bp, mask: _mla_attend(bn, bp, kb_nope, kpe_all, vb, mask)
    if past is None:
        oa = _sweep(fa, (q1, q2), T)
        ob = _sweep(fb, (qb_nope, qb_pe), T)
    else:
        oa = fa(q1, q2, None)
        ob = fb(qb_nope, qb_pe, None)
    oa = _rmsnorm(oa, p['g_subln']) * (1.0 - lam_init)
    ya = oa.reshape(B, T, H_A * DV_A) @ p['w_branch_a']
    yb = ob.reshape(B, T, H_B * DV_B) @ p['w_branch_b']
    g = jax.nn.sigmoid((gate + p['b_gate']).astype(f32)).astype(x.dtype).reshape(B, T, 2, D_MODEL)
    h = x + (g[:, :, 0] * ya + g[:, :, 1] * yb) @ p['w_out']
    hn = _rmsnorm(h, p['g_ffn'])
    u_g = hn @ p['w_ff_gate']
    u = hn @ p['w_ff_up']
    gp = jnp.concatenate([conv_past, u_g], axis=1)
    c = p['b_conv'] + sum(p['w_conv'][j] * gp[:, j:j + T] for j in range(CONV_W))
    y = h + (jax.nn.silu(c) * u) @ p['w_ff_down']
    return y, (ka, va, ckv, kpe, gp[:, T:])


def setup_inputs(seed: int = 0) -> dict:
    key = jax.random.key(seed)
    ks = jax.random.split(key, 40)
    f32 = jnp.float32
    nrm = lambda k, shape, s: jax.random.normal(k, shape, f32) * s
    gain = lambda k, n: 1.0 + 0.02 * jax.random.normal(k, (DEPTH, n), f32)
    return {
        'x_prompt': nrm(ks[0], (BATCH, SEQ, D_MODEL), 1.0),
        'x_sample': nrm(ks[1], (DEC_BATCH, DEC_SEQ, D_MODEL), 1.0),
        'cache_dk': nrm(ks[2], (DEPTH, DEC_BATCH, PAST_LEN, 2 * H_A, DH_A), 1.0),
        'cache_dv': nrm(ks[3], (DEPTH, DEC_BATCH, PAST_LEN, H_A, DV_A), 1.0),
        'cache_ckv': nrm(ks[4], (DEPTH, DEC_BATCH, PAST_LEN, KV_LORA), 1.0),
        'cache_kpe': nrm(ks[5], (DEPTH, DEC_BATCH, PAST_LEN, D_ROPE), 1.0),
        'state_conv': nrm(ks[6], (DEPTH, DEC_BATCH, CONV_W - 1, D_FF), 1.0),
        'g_attn': gain(ks[7], D_MODEL),
        'w_in': nrm(ks[8], (DEPTH, D_MODEL, N_IN), D_MODEL ** -0.5),
        'lam_q1': nrm(ks[9], (DEPTH, DH_A), 0.1),
        'lam_k1': nrm(ks[10], (DEPTH, DH_A), 0.1),
        'lam_q2': nrm(ks[11], (DEPTH, DH_A), 0.1),
        'lam_k2': nrm(ks[12], (DEPTH, DH_A), 0.1),
        'g_subln': gain(ks[13], DV_A),
        'w_branch_a': nrm(ks[14], (DEPTH, H_A * DV_A, D_MODEL), (H_A * DV_A) ** -0.5),
        'g_qa': gain(ks[15], Q_LORA),
        'w_qb': nrm(ks[16], (DEPTH, Q_LORA, H_B * (D_NOPE + D_ROPE)), Q_LORA ** -0.5),
        'g_kva': gain(ks[17], KV_LORA),
        'w_kvb': nrm(ks[18], (DEPTH, KV_LORA, H_B * (D_NOPE + DV_B)), KV_LORA ** -0.5),
        'w_branch_b': nrm(ks[19], (DEPTH, H_B * DV_B, D_MODEL), (H_B * DV_B) ** -0.5),
        'b_gate': nrm(ks[20], (DEPTH, N_GATE), 0.02),
        'w_out': nrm(ks[21], (DEPTH, D_MODEL, D_MODEL), D_MODEL ** -0.5),
        'g_ffn': gain(ks[22], D_MODEL),
        'w_ff_gate': nrm(ks[23], (DEPTH, D_MODEL, D_FF), D_MODEL ** -0.5),
        'w_ff_up': nrm(ks[24], (DEPTH, D_MODEL, D_FF), D_MODEL ** -0.5),
        'w_conv': nrm(ks[25], (DEPTH, CONV_W, D_FF), CONV_W ** -0.5),
        'b_conv': nrm(ks[26], (DEPTH, D_FF), 0.02),
        'w_ff_down': nrm(ks[27], (DEPTH, D_FF, D_MODEL), D_FF ** -0.5),
        'g_final': 1.0 + 0.02 * jax.random.normal(ks[28], (D_MODEL,), f32),
    }


def reference(x_prompt, x_sample, cache_dk, cache_dv, cache_ckv, cache_kpe, state_conv,
              g_attn, w_in, lam_q1, lam_k1, lam_q2, lam_k2, g_subln, w_branch_a,
              g_qa, w_qb, g_kva, w_kvb, w_branch_b, b_gate, w_out,
              g_ffn, w_ff_gate, w_ff_up, w_conv, b_conv, w_ff_down, g_final):
    T_p = x_prompt.shape[1]
    T_s = x_sample.shape[1]
    P = cache_dk.shape[2]
    pos_p = jnp.arange(T_p)
    pos_s = P + jnp.arange(T_s)
    hp, hs = x_prompt, x_sample
    outs_p, outs_s = [], []
    for l in range(DEPTH):
        lam_init = 0.8 - 0.6 * math.exp(-0.3 * l)
        p = dict(g_attn=g_attn[l], w_in=w_in[l], lam_q1=lam_q1[l], lam_k1=lam_k1[l],
                 lam_q2=lam_q2[l], lam_k2=lam_k2[l], g_subln=g_subln[l], w_branch_a=w_branch_a[l],
                 g_qa=g_qa[l], w_qb=w_qb[l], g_kva=g_kva[l], w_kvb=w_kvb[l], w_branch_b=w_branch_b[l],
                 b_gate=b_gate[l], w_out=w_out[l], g_ffn=g_ffn[l], w_ff_gate=w_ff_gate[l],
                 w_ff_up=w_ff_up[l], w_conv=w_conv[l], b_conv=b_conv[l], w_ff_down=w_ff_down[l])
        hp, st_p = _layer(hp, pos_p, None, lam_init, p)
        past = dict(dk=cache_dk[l], dv=cache_dv[l], ckv=cache_ckv[l], kpe=cache_kpe[l], conv=state_conv[l])
        hs, st_s = _layer(hs, pos_s, past, lam_init, p)
        outs_p.append(st_p)
        outs_s.append(st_s)
    stk = lambda outs, i: jnp.stack([o[i] for o in outs], axis=0)
    y_prompt = _rmsnorm(hp, g_final)
    y_sample = _rmsnorm(hs, g_final)
    return (y_prompt, y_sample,
            stk(outs_p, 0), stk(outs_p, 1), stk(outs_p, 2), stk(outs_p, 3), stk(outs_p, 4),
            stk(outs_s, 0), stk(outs_s, 1), stk(outs_s, 2), stk(outs_s, 3), stk(outs_s, 4))
```

```python
import math
from contextlib import ExitStack
import numpy as np
import ml_dtypes
import concourse.bass as bass
import concourse.mybir as mybir
from concourse.bass_utils import run_bass_kernel_spmd

F32 = mybir.dt.float32
BF16 = mybir.dt.bfloat16
AF = mybir.ActivationFunctionType
ALU = mybir.AluOpType
AX = mybir.AxisListType
NS = 8
EPS = 1e-6
NEGB = -60.0
SCR_KIND = "Internal"
ROPE_THETA = 500000.0

FULL = dict(D=4096, SEQ=8192, PAST=2048, HA=16, HB=16, QL=896, KVL=512, FF=11008)


class Res:
    __slots__ = ("name", "w", "rd", "lane")

    def __init__(self, name):
        self.name = name
        self.w = None
        self.rd = {}
        self.lane = None


class Lane:
    __slots__ = ("key", "sem", "k")


class TR:
    def __init__(self, nc, stack, n_lanes=None):
        n_lanes = len(nc.free_semaphores) - 3 * NS - 2
        self.nc = nc
        self.engs = {}
        for e in ("pe", "act", "dve", "pool", "sp"):
            sems = [stack.enter_context(nc.semaphore(f"s_{e}_{i}")) for i in range(NS)] if e in ("pe", "act", "dve") else []
            self.engs[e] = dict(sems=sems, n=0, ops=[], known={})
        self.free_lanes = [stack.enter_context(nc.semaphore(f"s_l{i}")) for i in range(n_lanes)]
        self.lanes = []

    def lane_of(self, res):
        if res.lane is None:
            ln = Lane()
            ln.key = "L%d" % len(self.lanes)
            ln.sem = self.free_lanes.pop()
            ln.k = 0
            self.lanes.append(ln)
            res.lane = ln
        return res.lane

    def op(self, eng, fn, reads=(), writes=(), lane=None):
        E = self.engs[eng]
        waits = {}

        def need(ev):
            if ev is None:
                return
            key, idx = ev[0], ev[1]
            if key == "pe" and eng == "pe":
                return
            if E["known"].get(key, -1) >= idx:
                return
            if key not in waits or waits[key][1] < idx:
                waits[key] = ev

        for r in reads:
            need(r.w)
        for w in writes:
            need(w.w)
            for ev in w.rd.values():
                need(ev)
        for key, ev in waits.items():
            E["known"][key] = ev[1]
        if fn is None:
            E["ops"].append((list(waits.values()), None, None, 0))
            return
        if lane is None:
            n = E["n"]
            E["n"] += 1
            ev = (eng, n, E["sems"][n % NS], n // NS + 1)
            inc = 1
        else:
            lane.k += 1
            ev = (lane.key, lane.k, lane.sem, 16 * lane.k)
            inc = 16
        E["ops"].append((list(waits.values()), fn, ev[2], inc))
        for r in reads:
            r.rd[ev[0]] = ev
        for w in writes:
            w.w = ev
            w.rd = {}

    def dma(self, q, out, in_, reads=(), writes=(), lane_res=None):
        ln = self.lane_of(lane_res)
        self.op(q, lambda e, o=out, i=in_: e.dma_start(out=o, in_=i), reads, writes, lane=ln)

    def barrier(self):
        evs = []
        for name, E in self.engs.items():
            if E["n"] > 0:
                n = E["n"] - 1
                evs.append((name, n, E["sems"][n % NS], n // NS + 1))
        for ln in self.lanes:
            if ln.k > 0:
                evs.append((ln.key, ln.k, ln.sem, 16 * ln.k))
        for name, E in self.engs.items():
            waits = []
            for ev in evs:
                if E["known"].get(ev[0], -1) < ev[1]:
                    waits.append(ev)
                    E["known"][ev[0]] = ev[1]
            E["ops"].append((waits, None, None, 0))

    def emit(self, block):
        def mk(name):
            def f(e):
                for waits, fn, sem, inc in self.engs[name]["ops"]:
                    for ev in waits:
                        e.wait_ge(ev[2], ev[3])
                    if fn is not None:
                        fn(e).then_inc(sem, inc)
            return f
        block.tensor(mk("pe"))
        block.scalar(mk("act"))
        block.vector(mk("dve"))
        block.gpsimd(mk("pool"))
        block.sync(mk("sp"))


def build(cfg):
    D, SEQ, PAST = cfg["D"], cfg["SEQ"], cfg["PAST"]
    HA, HB, QL, KVL, FF = cfg["HA"], cfg["HB"], cfg["QL"], cfg["KVL"], cfg["FF"]
    TB = SEQ // 8
    NQT = TB // 128 + 1
    NKT = SEQ // 128
    NPQ = NQT * 128
    NQ = NPQ + 64
    KC = D // 128
    NKA, NVA, NVB, NKB = 2 * HA * 64, HA * 128, HB * 128, HB * 128
    NQB = HB * 192
    NF = FF // 128
    o_q, o_k, o_v = 0, NKA, 2 * NKA
    o_cq = 2 * NKA + NVA
    o_ckv = o_cq + QL
    o_g = o_ckv + KVL + 64
    N_IN = o_g + 2 * D
    TKS = PAST + 64
    NKTS = (TKS + 127) // 128
    SC_A = 64 ** -0.5
    SC_B = 192 ** -0.5
    LAM_INIT = 0.8 - 0.6 * math.exp(0.0)

    nc = bass.Bass("TRN2", target_bir_lowering=False)

    def din(name, shape, dt=F32):
        return nc.dram_tensor(name, list(shape), dt, kind="ExternalInput").ap()

    def dout(name, shape):
        return nc.dram_tensor(name, list(shape), F32, kind="ExternalOutput").ap()

    def dscr(name, shape, dt=BF16):
        return nc.dram_tensor(name, list(shape), dt, kind=SCR_KIND).ap()

    I = dict(
        xk=din("xk", [SEQ, D]), xs=din("xs", [64, D]),
        w_in=din("w_in", [D, N_IN]), w_qb=din("w_qb", [QL, NQB]),
        w_kvn=din("w_kvn", [KVL, NKB]), w_kvv=din("w_kvv", [KVL, NVB]),
        w_ba=din("w_ba", [NVA, D]), w_bb=din("w_bb", [NVB, D]), w_out=din("w_out", [D, D]),
        w_fg=din("w_fg", [D, FF]), w_fu=din("w_fu", [D, FF]), w_fd=din("w_fd", [FF, D]),
        g_attn=din("g_attn", [128, D]), g_ffn=din("g_ffn", [128, D]), g_final=din("g_final", [128, D]),
        g_qa=din("g_qa", [128, QL]), g_kva=din("g_kva", [128, KVL]), g_sub=din("g_sub", [128, 128]),
        b_gate=din("b_gate", [128, 2 * KC]), w_conv=din("w_conv", [128, 3 * NF]), b_conv=din("b_conv", [128, NF]),
        lam=din("lam", [128, 256]),
        c_dk=din("c_dk", [PAST, NKA]), c_dv=din("c_dv", [PAST, NVA]),
        c_ckv=din("c_ckv", [PAST, KVL]), c_kpe=din("c_kpe", [PAST, 64]), st_conv=din("st_conv", [128, 2 * NF]),
        cosA=din("cosA", [SEQ + 64, 2 * HA * 8]), sinA=din("sinA", [SEQ + 64, 2 * HA * 8]),
        cosB=din("cosB", [SEQ + 64, HB * 32]), sinB=din("sinB", [SEQ + 64, HB * 32]),
        visb=din("visb", [128, NKT]), flag=din("flag", [128, 1]),
        ident=din("ident", [128, 128], BF16), identf=din("identf", [128, 128]),
    )
    O = dict(
        y_p=dout("y_p", [TB, D]), y_s=dout("y_s", [64, D]),
        dk_p=dout("dk_p", [TB, NKA]), dv_p=dout("dv_p", [TB, NVA]),
        ckv_p=dout("ckv_p", [TB, KVL]), kpe_p=dout("kpe_p", [TB, 64]), conv_p=dout("conv_p", [2, FF]),
        dk_s=dout("dk_s", [64, NKA]), dv_s=dout("dv_s", [64, NVA]),
        ckv_s=dout("ckv_s", [64, KVL]), kpe_s=dout("kpe_s", [64, 64]), conv_s=dout("conv_s", [2, FF]),
    )
    S = dict(
        KT_p=dscr("KT_p", [NKA, SEQ]), V_p=dscr("V_p", [SEQ, NVA]),
        KBT_p=dscr("KBT_p", [NKB, SEQ]), KPET_p=dscr("KPET_p", [64, SEQ]), VB_p=dscr("VB_p", [SEQ, NVB]),
        KT_s=dscr("KT_s", [NKA, NKTS * 128]), V_s=dscr("V_s", [NKTS * 128, NVA]),
        KBT_s=dscr("KBT_s", [NKB, NKTS * 128]), KPET_s=dscr("KPET_s", [64, NKTS * 128]),
        VB_s=dscr("VB_s", [NKTS * 128, NVB]),
        QT=dscr("QT", [NKA, NQ]), QBN=dscr("QBN", [HB * 128, NQ]), QBP=dscr("QBP", [HB * 64, NQ]),
        GT=dscr("GT", [2 * D, NQ]), OA=dscr("OA", [NQ, NVA]), OB=dscr("OB", [NQ, NVB]),
    )

    with ExitStack() as stack:
        tr = TR(nc, stack)
        ec = stack.enter_context

        def sb(name, shape, dt=F32, st=None):
            return (st or stack).enter_context(nc.sbuf_tensor("sb_" + name, list(shape), dt))

        def ps(name, shape, dt=F32, st=None):
            return (st or stack).enter_context(nc.psum_tensor("ps_" + name, list(shape), dt))

        ident = sb("ident", [128, 128], BF16)
        r_const = Res("const")
        tr.dma("sp", ident[:], I["ident"][:], writes=[r_const], lane_res=r_const)

        def act_copy(out, in_, reads, writes, eng="act"):
            if eng == "act":
                tr.op("act", lambda e, o=out, i=in_: e.activation(out=o, in_=i, func=AF.Copy), reads, writes)
            else:
                tr.op("dve", lambda e, o=out, i=in_: e.tensor_copy(out=o, in_=i), reads, writes)

        cp_rr = [0]

        def any_copy(out, in_, reads, writes):
            cp_rr[0] ^= 1
            act_copy(out, in_, reads, writes, "act" if cp_rr[0] else "dve")

        def rstd_of(src, rows, n, junk, r_junk, ss, rstd, r_src, r_tmp):
            tr.op("dve", lambda e: e.memset(ss[:rows], 0.0), [], [r_tmp])
            tr.op("act", lambda e: e.activation(out=junk[:rows, :n], in_=src, func=AF.Square,
                                                accum_out=ss[:rows]), [r_src], [r_tmp, r_junk])
            tr.op("dve", lambda e: e.tensor_scalar(out=rstd[:rows], in0=ss[:rows], scalar1=1.0 / n, scalar2=EPS,
                                                   op0=ALU.mult, op1=ALU.add), [r_tmp], [r_tmp])
            tr.op("act", lambda e: e.activation(out=rstd[:rows], in_=rstd[:rows], func=AF.Sqrt), [r_tmp], [r_tmp])
            tr.op("dve", lambda e: e.reciprocal(out=rstd[:rows], in_=rstd[:rows]), [r_tmp], [r_tmp])

        def rope(buf, rows, nh, hd, half, cos, sin, tmp, r_buf, r_tab, r_tmp, off=0):
            v = buf[:rows, :nh * hd].rearrange("p (h d) -> p h d", d=hd)
            x1 = v[:, :, off:off + half]
            x2 = v[:, :, off + half:off + 2 * half]
            c = cos[:rows, :nh * half].rearrange("p (h d) -> p h d", d=half)
            s = sin[:rows, :nh * half].rearrange("p (h d) -> p h d", d=half)
            n = nh * half
            t = [tmp[:rows, i * n:(i + 1) * n].rearrange("p (h d) -> p h d", d=half) for i in range(4)]
            for (o, a, b) in ((t[0], x1, c), (t[1], x2, s), (t[2], x2, c), (t[3], x1, s)):
                tr.op("dve", lambda e, o=o, a=a, b=b: e.tensor_tensor(out=o, in0=a, in1=b, op=ALU.mult),
                      [r_buf, r_tab], [r_tmp])
            tr.op("dve", lambda e: e.tensor_tensor(out=x1, in0=t[0], in1=t[1], op=ALU.subtract), [r_tmp], [r_buf])
            tr.op("dve", lambda e: e.tensor_tensor(out=x2, in0=t[2], in1=t[3], op=ALU.add), [r_tmp], [r_buf])

        def transposes(src, rows, nch, pT, r_pT, dst_fn, r_src, r_dst, chw=128, stride=None, off=0):
            stride = stride or chw
            j0 = 0
            while j0 < nch:
                n = min(8, nch - j0)

                pT, r_pT = nxt_pT()

                def f(e, j0=j0, n=n, pT=pT):
                    ins = None
                    for j in range(n):
                        ins = e.transpose(out=pT[:chw, j, :rows], in_=src[:rows, (j0 + j) * stride + off:(j0 + j) * stride + off + chw],
                                          identity=ident[:rows, :rows])
                    return ins
                tr.op("pe", f, [r_src, r_const], [r_pT])
                any_copy(dst_fn(j0, n), pT[:chw, 0:n, :rows], [r_pT], [r_dst])
                j0 += n

        NWS = 2
        WB = 256
        WBv = [WB]
        wslots = [None] * NWS
        r_ws = [[Res(f"wslot{i}a"), Res(f"wslot{i}b")] for i in range(NWS)]
        ws_gen = [0]

        def set_wslots(st, wb):
            ws_gen[0] += 1
            WBv[0] = wb
            for i in range(NWS):
                wslots[i] = sb(f"wslot{ws_gen[0]}_{i}", [128, 32, wb], BF16, st)
        ws_rr = [0]

        wcache = {}
        WCACHE = cfg.get("wcache", True)

        def cached_load(key, dst, src_fp32, shape, writes, lane_res):
            if key in wcache:
                scr, r_scr = wcache[key]
                tr.dma("pool", dst, scr, reads=[r_scr], writes=writes, lane_res=lane_res)
                return
            tr.dma("pool", dst, src_fp32, writes=writes, lane_res=lane_res)
            if WCACHE:
                scr = dscr("wc%d" % len(wcache), shape)
                r_scr = Res("wc%d" % len(wcache))
                tr.dma("sp", scr, dst, reads=writes, writes=[r_scr], lane_res=lane_res)
                wcache[key] = (scr, r_scr)

        def load_w(W, k0, nk, c0, w):
            i = ws_rr[0] % NWS
            ws_rr[0] += 1
            rows = W.shape[0]
            kk = min(128, rows - k0 * 128) if nk == 1 else 128
            src = W[k0 * 128:k0 * 128 + nk * kk, c0:c0 + w].rearrange("(k p) c -> p k c", p=kk)
            cached_load((id(W), k0, nk, c0, w), wslots[i][:kk, :nk, :w], src, [kk, nk, w], r_ws[i], r_ws[i][0])
            return wslots[i], r_ws[i]

        def gemm_tm(xT, r_x, kcn, tiles, W, c0, ncols, accs, r_accs, evac, wb=None):
            wb = wb or WBv[0]
            b0 = 0
            while b0 < ncols:
                w = min(wb, ncols - b0)
                nkb = (kcn + 31) // 32
                for kb in range(nkb):
                    k0 = kb * 32
                    nk = min(32, kcn - k0)
                    wt, r_w = load_w(W, k0, nk, c0 + b0, w)
                    for ti, (t0, rows) in enumerate(tiles):
                        acc, r_acc = accs[ti], r_accs[ti]

                        def f(e, wt=wt, t0=t0, rows=rows, k0=k0, nk=nk, w=w, acc=acc, kb=kb):
                            ins = None
                            for k in range(nk):
                                ins = e.matmul(acc[:rows, :w], lhsT=xT[:, k0 + k, t0:t0 + rows], rhs=wt[:, k, :w],
                                               start=(kb == 0 and k == 0), stop=(kb == nkb - 1 and k == nk - 1))
                            return ins
                        tr.op("pe", f, [r_x] + r_w, [r_acc])
                for ti, (t0, rows) in enumerate(tiles):
                    evac(ti, b0, w, accs[ti], r_accs[ti])
                b0 += w

        def gemm_fm(xT, r_x, kcn, ntok, W, c0, ncols, accs, r_accs, evac, kpart=128):
            b0 = 0
            mi = 0
            while b0 < ncols:
                w = min(WBv[0], ncols - b0)
                wt, r_w = load_w(W, 0, kcn, c0 + b0, w)
                for m0 in range(0, w, 128):
                    mw = min(128, w - m0)
                    acc, r_acc = accs[mi % len(accs)], r_accs[mi % len(accs)]

                    def f(e, wt=wt, m0=m0, mw=mw, acc=acc):
                        ins = None
                        for k in range(kcn):
                            ins = e.matmul(acc[:mw, :ntok], lhsT=wt[:kpart, k, m0:m0 + mw], rhs=xT[:kpart, k, :ntok],
                                           start=(k == 0), stop=(k == kcn - 1))
                        return ins
                    tr.op("pe", f, [r_x] + r_w, [r_acc])
                    evac((b0 + m0) // 128, mw, acc, r_acc)
                    mi += 1
                b0 += w

        pacc = [ps(f"pacc{i}", [128, 512], F32) for i in range(4)]; r_pacc = [Res(f"pacc{i}") for i in range(4)]
        pT = [ps(f"pT{i}", [128, 8, 128], BF16) for i in range(2)]; r_pT = [Res(f"pT{i}") for i in range(2)]
        pt_rr = [0]

        def nxt_pT():
            pt_rr[0] ^= 1
            return pT[pt_rr[0]], r_pT[pt_rr[0]]

        def phase1():
            with ExitStack() as st1_:
                st1 = stack if cfg.get('noalias') else st1_
                set_wslots(st1, cfg.get('wb1', 512))
                g_attn = sb("g_attn", [128, D], F32, st1)
                g_kva = sb("g_kva", [128, KVL], F32, st1)
                tr.dma("sp", g_attn[:], I["g_attn"][:], writes=[r_const], lane_res=r_const)
                tr.dma("sp", g_kva[:], I["g_kva"][:], writes=[r_const], lane_res=r_const)
                x_tm = sb("x_tm", [128, D], F32, st1); r_x = Res("x_tm")
                xs_bf = sb("xs_bf", [128, D], BF16, st1); r_xs = Res("xs_bf")
                xnT = sb("xnT", [128, KC, 512], BF16, st1); r_xnT = Res("xnT")
                ss = sb("ss", [128, 1], F32, st1); rstd = sb("rstd", [128, 1], F32, st1); r_st = Res("stat")
                ss2 = sb("ss2", [128, 1], F32, st1); rstd2 = sb("rstd2", [128, 1], F32, st1); r_st2 = Res("stat2")
                kf = [sb(f"kf{i}", [128, max(NKA, NVA, KVL + 64)], F32, st1) for i in range(4)]; r_kf = [Res(f"kf{i}") for i in range(4)]
                vf = kf; r_vf = r_kf
                cf = kf; r_cf = r_kf
                kbf = sb("kbf", [128, max(NKA, NVA, NVB)], BF16, st1); r_kbf = Res("kbf")
                vbf = kbf; r_vbf = r_kbf
                cbf = sb("cbf", [128, KVL + 64], BF16, st1); r_cbf = Res("cbf")
                tabA = sb("tabA", [128, 2 * 2 * HA * 8], F32, st1); r_tabA = Res("tabA")
                tabB = sb("tabB", [128, 2 * 32], F32, st1); r_tabB = Res("tabB")
                rtmp = sb("rtmp", [128, 4 * 2 * HA * 8], F32, st1); r_rtmp = Res("rtmp")
                stage = sb("stage", [128, max(NKA, NKB) // 128, 512], BF16, st1); r_stage = Res("stage")
                stage2 = stage; r_stage2 = r_stage
                ckvT = sb("ckvT", [128, KVL // 128, 512], BF16, st1); r_ckvT = Res("ckvT")
                kpeT = sb("kpeT", [64, 1, 512], BF16, st1); r_kpeT = Res("kpeT")
                vbb = kbf; r_vbb = r_kbf
                def make_xnT(tiles_src, g_bc):
                    for ti, (src, rows) in enumerate(tiles_src):
                        tr.dma("sp", x_tm[:rows], src, writes=[r_x], lane_res=r_x)
                        rstd_of(x_tm[:rows, :], rows, D, xs_bf, r_xs, ss, rstd, r_x, r_st)
                        tr.op("dve", lambda e, rows=rows: e.scalar_tensor_tensor(
                            out=xs_bf[:rows], in0=x_tm[:rows], scalar=rstd[:rows, 0:1], in1=g_bc[:rows],
                            op0=ALU.mult, op1=ALU.mult), [r_x, r_st, r_const], [r_xs])
                        p, rp = nxt_pT()
                        transposes(xs_bf, rows, KC, p, rp,
                                   lambda j0, n, ti=ti, rows=rows: xnT[:, j0:j0 + n, ti * 128:ti * 128 + rows],
                                   r_xs, r_xnT)

                def k_post(ti, rows, pos0, c0, outs, orow):
                    tr.dma("sp", tabA[:rows, 0:2 * HA * 8], I["cosA"][pos0:pos0 + rows, :], writes=[r_tabA], lane_res=r_tabA)
                    tr.dma("sp", tabA[:rows, 2 * HA * 8:], I["sinA"][pos0:pos0 + rows, :], writes=[r_tabA], lane_res=r_tabA)
                    rope(kf[ti], rows, 2 * HA, 64, 8, tabA[:, 0:2 * HA * 8], tabA[:, 2 * HA * 8:], rtmp, r_kf[ti], r_tabA, r_rtmp)
                    if outs is not None:
                        tr.dma("sp", outs["dk"][orow:orow + rows, :], kf[ti][:rows, :NKA], reads=[r_kf[ti]], lane_res=r_kf[ti])
                    act_copy(kbf[:rows, :NKA], kf[ti][:rows, :NKA], [r_kf[ti]], [r_kbf])
                    p, rp = nxt_pT()
                    transposes(kbf, rows, NKA // 128, p, rp,
                               lambda j0, n: stage[:, j0:j0 + n, c0:c0 + rows], r_kbf, r_stage)

                def v_post(ti, rows, outs, orow):
                    if outs is not None:
                        tr.dma("sp", outs["dv"][orow:orow + rows, :], vf[ti][:rows, :NVA], reads=[r_vf[ti]], lane_res=r_vf[ti])
                    act_copy(vbf[:rows, :NVA], vf[ti][:rows, :NVA], [r_vf[ti]], [r_vbf], "dve")

                def c_post(ti, rows, pos0, outs, orow):
                    rstd_of(cf[ti][:rows, :KVL], rows, KVL, cbf, r_cbf, ss2, rstd2, r_cf[ti], r_st2)
                    tr.op("dve", lambda e: e.scalar_tensor_tensor(
                        out=cf[ti][:rows, :KVL], in0=cf[ti][:rows, :KVL], scalar=rstd2[:rows, 0:1], in1=g_kva[:rows],
                        op0=ALU.mult, op1=ALU.mult), [r_st2, r_const], [r_cf[ti]])
                    tr.dma("sp", tabB[:rows, 0:32], I["cosB"][pos0:pos0 + rows, 0:32], writes=[r_tabB], lane_res=r_tabB)
                    tr.dma("sp", tabB[:rows, 32:64], I["sinB"][pos0:pos0 + rows, 0:32], writes=[r_tabB], lane_res=r_tabB)
                    rope(cf[ti][:, KVL:KVL + 64], rows, 1, 64, 32, tabB[:, 0:32], tabB[:, 32:64], rtmp, r_cf[ti], r_tabB, r_rtmp)
                    if outs is not None:
                        tr.dma("sp", outs["ckv"][orow:orow + rows, :], cf[ti][:rows, :KVL], reads=[r_cf[ti]], lane_res=r_cf[ti])
                        tr.dma("sp", outs["kpe"][orow:orow + rows, :], cf[ti][:rows, KVL:KVL + 64], reads=[r_cf[ti]], lane_res=r_cf[ti])
                    act_copy(cbf[:rows], cf[ti][:rows, :KVL + 64], [r_cf[ti]], [r_cbf])

                def latent_post(rows, c0):
                    p, rp = nxt_pT()
                    transposes(cbf, rows, KVL // 128, p, rp,
                               lambda j0, n: ckvT[:, j0:j0 + n, c0:c0 + rows], r_cbf, r_ckvT)
                    p, rp = nxt_pT()
                    transposes(cbf[:, KVL:], rows, 1, p, rp,
                               lambda j0, n: kpeT[:, 0:1, c0:c0 + rows], r_cbf, r_kpeT, chw=64)

                def kvb_group(tiles, ntok, sc, col0):
                    def ev_n(m, mw, acc, r_acc):
                        any_copy(stage2[:mw, m, :ntok], acc[:mw, :ntok], [r_acc], [r_stage2])
                    gemm_fm(ckvT, r_ckvT, KVL // 128, ntok, I["w_kvn"], 0, NKB, pacc, r_pacc, ev_n)
                    tr.dma("sp", sc["KBT"][:, col0:col0 + ntok].rearrange("(m p) t -> p m t", p=128),
                           stage2[:, :NKB // 128, :ntok], reads=[r_stage2], lane_res=r_stage2)

                    def ev_v(ti, b0, w, acc, r_acc):
                        t0, rows = tiles[ti]
                        any_copy(vbb[:rows, b0:b0 + w], acc[:rows, :w], [r_acc], [r_vbb])
                        if b0 + w >= NVB:
                            tr.dma("sp", sc["VB"][col0 + t0:col0 + t0 + rows, :], vbb[:rows, :NVB], reads=[r_vbb], lane_res=r_vbb)
                    for ti, (t0, rows) in enumerate(tiles):
                        gemm_tm(ckvT, r_ckvT, KVL // 128, [(t0, rows)], I["w_kvv"], 0, NVB, [pacc[ti]], [r_pacc[ti]],
                                lambda _ti, b0, w, acc, r_acc, ti=ti: ev_v(ti, b0, w, acc, r_acc))

                def k_group(tiles_src, pos0, sc, col0, outs, out_rows):
                    tiles = [(i * 128, rows) for i, (_, rows) in enumerate(tiles_src)]
                    ntok = tiles[-1][0] + tiles[-1][1]
                    make_xnT(tiles_src, g_attn)

                    def ev_k(ti, b0, w, acc, r_acc):
                        act_copy(kf[ti][:tiles[ti][1], b0:b0 + w], acc[:tiles[ti][1], :w], [r_acc], [r_kf[ti]])

                    def ev_v(ti, b0, w, acc, r_acc):
                        act_copy(vf[ti][:tiles[ti][1], b0:b0 + w], acc[:tiles[ti][1], :w], [r_acc], [r_vf[ti]], "dve")

                    def ev_c(ti, b0, w, acc, r_acc):
                        act_copy(cf[ti][:tiles[ti][1], b0:b0 + w], acc[:tiles[ti][1], :w], [r_acc], [r_cf[ti]])
                    def oo(ti):
                        o = outs if (outs is not None and out_rows[ti] is not None) else None
                        return o, (out_rows[ti] if o is not None else 0)
                    gemm_tm(xnT, r_xnT, KC, tiles, I["w_in"], o_k, NKA, pacc, r_pacc, ev_k)
                    for ti, (t0, rows) in enumerate(tiles):
                        k_post(ti, rows, pos0 + t0, t0, *oo(ti))
                    tr.dma("sp", sc["KT"][:, col0:col0 + ntok].rearrange("(m p) t -> p m t", p=128),
                           stage[:, :NKA // 128, :ntok], reads=[r_stage], lane_res=r_stage)
                    gemm_tm(xnT, r_xnT, KC, tiles, I["w_in"], o_v, NVA, pacc, r_pacc, ev_v)
                    for ti, (t0, rows) in enumerate(tiles):
                        v_post(ti, rows, *oo(ti))
                        tr.dma("sp", sc["V"][col0 + t0:col0 + t0 + rows, :], vbf[:rows, :NVA], reads=[r_vbf], lane_res=r_vbf)
                    gemm_tm(xnT, r_xnT, KC, tiles, I["w_in"], o_ckv, KVL + 64, pacc, r_pacc, ev_c)
                    for ti, (t0, rows) in enumerate(tiles):
                        c_post(ti, rows, pos0 + t0, *oo(ti))
                        latent_post(rows, t0)
                    tr.dma("sp", sc["KPET"][:, col0:col0 + ntok], kpeT[:, 0, :ntok], reads=[r_kpeT], lane_res=r_kpeT)
                    kvb_group(tiles, ntok, sc, col0)

                scP = dict(KT=S["KT_p"], V=S["V_p"], KBT=S["KBT_p"], KPET=S["KPET_p"], VB=S["VB_p"])
                scS = dict(KT=S["KT_s"], V=S["V_s"], KBT=S["KBT_s"], KPET=S["KPET_s"], VB=S["VB_s"])
                outP = dict(dk=O["dk_p"], dv=O["dv_p"], ckv=O["ckv_p"], kpe=O["kpe_p"])
                outS = dict(dk=O["dk_s"], dv=O["dv_s"], ckv=O["ckv_s"], kpe=O["kpe_s"])
                for g in range(NKT // 4):
                    tl = [(I["xk"][(4 * g + i) * 128:(4 * g + i + 1) * 128, :], 128) for i in range(4)]
                    orows = [((4 * g + i - 1) * 128 if 1 <= 4 * g + i < NQT else None) for i in range(4)]
                    k_group(tl, g * 512, scP, g * 512, outP, orows)
                k_group([(I["xs"][:, :], 64)], SEQ, scS, PAST, outS, [0])
                for g in range(PAST // 512 if PAST >= 512 else 1):
                    gt = min(4, PAST // 128)
                    for i in range(gt):
                        r0 = (g * 4 + i) * 128
                        tr.dma("pool", kbf[:, :NKA], I["c_dk"][r0:r0 + 128, :], writes=[r_kbf], lane_res=r_kbf)
                        p, rp = nxt_pT()
                        transposes(kbf, 128, NKA // 128, p, rp,
                                   lambda j0, n, i=i: stage[:, j0:j0 + n, i * 128:(i + 1) * 128], r_kbf, r_stage)
                        tr.dma("pool", vbf[:, :NVA], I["c_dv"][r0:r0 + 128, :], writes=[r_vbf], lane_res=r_vbf)
                        tr.dma("sp", S["V_s"][r0:r0 + 128, :], vbf[:, :NVA], reads=[r_vbf], lane_res=r_vbf)
                        tr.dma("pool", cbf[:, :KVL], I["c_ckv"][r0:r0 + 128, :], writes=[r_cbf], lane_res=r_cbf)
                        tr.dma("pool", cbf[:, KVL:], I["c_kpe"][r0:r0 + 128, :], writes=[r_cbf], lane_res=r_cbf)
                        latent_post(128, i * 128)
                    nt = gt * 128
                    tr.dma("sp", S["KT_s"][:, g * 512:g * 512 + nt].rearrange("(m p) t -> p m t", p=128),
                           stage[:, :NKA // 128, :nt], reads=[r_stage], lane_res=r_stage)
                    tr.dma("sp", S["KPET_s"][:, g * 512:g * 512 + nt], kpeT[:, 0, :nt], reads=[r_kpeT], lane_res=r_kpeT)
                    kvb_group([(i * 128, 128) for i in range(gt)], nt, scS, g * 512)
                tr.barrier()


        phase1()
        PHASES = cfg.get("phases", 4)
        QTILES = [(I["xk"][j * 128:(j + 1) * 128, :], 128, j * 128, j * 128) for j in range(NQT)]
        QTILES.append((I["xs"][:, :], 64, SEQ, NPQ))
        QGROUPS = [QTILES[i:i + 4] for i in range(0, len(QTILES), 4)]
        def make_xnT2(tiles_src, g_bc, x_tm, r_x, xs_bf, r_xs, ss, rstd, r_st, xnT, r_xnT):
            for ti, (src, rows) in enumerate(tiles_src):
                tr.dma("sp", x_tm[:rows], src, writes=[r_x], lane_res=r_x)
                rstd_of(x_tm[:rows, :], rows, D, xs_bf, r_xs, ss, rstd, r_x, r_st)
                tr.op("dve", lambda e, rows=rows: e.scalar_tensor_tensor(
                    out=xs_bf[:rows], in0=x_tm[:rows], scalar=rstd[:rows, 0:1], in1=g_bc[:rows],
                    op0=ALU.mult, op1=ALU.mult), [r_x, r_st, r_const], [r_xs])
                p, rp = nxt_pT()
                transposes(xs_bf, rows, KC, p, rp,
                           lambda j0, n, ti=ti, rows=rows: xnT[:, j0:j0 + n, ti * 128:ti * 128 + rows], r_xs, r_xnT)

        def phase2():
            if PHASES >= 2:
              with ExitStack() as st2_:
                st2 = stack if cfg.get('noalias') else st2_
                set_wslots(st2, 256)
                g_attn = sb("g_attn2", [128, D], F32, st2)
                g_qa = sb("g_qa", [128, QL], F32, st2)
                b_gate = sb("b_gate", [128, 2 * KC], F32, st2)
                for t_, n_ in ((g_attn, "g_attn"), (g_qa, "g_qa"), (b_gate, "b_gate")):
                    if cfg.get("p2cut", 9) < 0:
                        continue
                    tr.dma("sp", t_[:], I[n_][:], writes=[r_const], lane_res=r_const)
                x_tm = sb("x_tm2", [128, D], F32, st2); r_x = Res("x_tm2")
                xs_bf = sb("xs_bf2", [128, D], BF16, st2); r_xs = Res("xs_bf2")
                xnT = sb("xnT2", [128, KC, 512], BF16, st2); r_xnT = Res("xnT2")
                ss = sb("ss_2", [128, 1], F32, st2); rstd = sb("rstd_2", [128, 1], F32, st2); r_st = Res("stat_2")
                qf = sb("qf", [128, max(NKA, NQB)], F32, st2); r_qf = Res("qf")
                qbf = sb("qbf", [128, max(NKA, NQB)], BF16, st2); r_qbf = Res("qbf")
                cqT = sb("cqT", [128, QL // 128, 128], BF16, st2); r_cqT = Res("cqT")
                tabA = sb("tabA2", [128, 2 * 2 * HA * 8], F32, st2); r_tabA = Res("tabA2")
                tabB = sb("tabB2", [128, 2 * HB * 32], F32, st2); r_tabB = Res("tabB2")
                rtmp = sb("rtmp2", [128, 4 * max(2 * HA * 8, HB * 32)], F32, st2); r_rtmp = Res("rtmp2")
                stQ = sb("stQ", [128, NKA // 128, 512], BF16, st2); r_stQ = Res("stQ")
                stN = sb("stN", [128, HB, 512], BF16, st2); r_stN = Res("stN")
                stP = sb("stP", [64, HB, 512], BF16, st2); r_stP = Res("stP")
                gbuf = [sb(f"gbuf{i}", [128, 512], BF16, st2) for i in range(2)]; r_gbuf = [Res(f"gbuf{i}") for i in range(2)]
                def do_group2(grp):
                    tiles = [(i * 128, rows) for i, (_, rows, _, _) in enumerate(grp)]
                    ntok = tiles[-1][0] + tiles[-1][1]
                    col0 = grp[0][3]
                    if cfg.get("p2cut", 9) < 1:
                        return
                    make_xnT2([(g_[0], g_[1]) for g_ in grp], g_attn, x_tm, r_x, xs_bf, r_xs, ss, rstd, r_st, xnT, r_xnT)

                    def ev_g(m, mw, acc, r_acc, ntok=ntok, col0=col0):
                        i = m % 2
                        tr.op("act", lambda e: e.activation(out=gbuf[i][:mw, :ntok], in_=acc[:mw, :ntok], func=AF.Sigmoid,
                                                            bias=b_gate[:mw, m:m + 1]), [r_acc, r_const], [r_gbuf[i]])
                        tr.dma("sp", S["GT"][m * 128:m * 128 + mw, col0:col0 + ntok], gbuf[i][:mw, :ntok],
                               reads=[r_gbuf[i]], lane_res=r_gbuf[i])
                    if cfg.get("p2cut", 9) >= 2:
                        gemm_fm(xnT, r_xnT, KC, ntok, I["w_in"], o_g, 2 * D, pacc, r_pacc, ev_g)
                    for ti, (t0, rows) in enumerate(tiles):
                        if cfg.get("p2cut", 9) < 3:
                            break
                        pos0 = grp[ti][2]
                        one = [(t0, rows)]
                        gemm_tm(xnT, r_xnT, KC, one, I["w_in"], o_q, NKA, [pacc[0]], [r_pacc[0]],
                                lambda _t, b0, w, acc, r_acc, rows=rows: act_copy(qf[:rows, b0:b0 + w], acc[:rows, :w], [r_acc], [r_qf]))
                        tr.dma("sp", tabA[:rows, 0:2 * HA * 8], I["cosA"][pos0:pos0 + rows, :], writes=[r_tabA], lane_res=r_tabA)
                        tr.dma("sp", tabA[:rows, 2 * HA * 8:], I["sinA"][pos0:pos0 + rows, :], writes=[r_tabA], lane_res=r_tabA)
                        rope(qf, rows, 2 * HA, 64, 8, tabA[:, 0:2 * HA * 8], tabA[:, 2 * HA * 8:], rtmp, r_qf, r_tabA, r_rtmp)
                        act_copy(qbf[:rows, :NKA], qf[:rows, :NKA], [r_qf], [r_qbf])
                        p, rp = nxt_pT()
                        transposes(qbf, rows, NKA // 128, p, rp,
                                   lambda j0, n, t0=t0, rows=rows: stQ[:, j0:j0 + n, t0:t0 + rows], r_qbf, r_stQ)
                        gemm_tm(xnT, r_xnT, KC, one, I["w_in"], o_cq, QL, [pacc[1]], [r_pacc[1]],
                                lambda _t, b0, w, acc, r_acc, rows=rows: act_copy(qf[:rows, b0:b0 + w], acc[:rows, :w], [r_acc], [r_qf]))
                        rstd_of(qf[:rows, :QL], rows, QL, qbf, r_qbf, ss, rstd, r_qf, r_st)
                        tr.op("dve", lambda e, rows=rows: e.scalar_tensor_tensor(
                            out=qbf[:rows, :QL], in0=qf[:rows, :QL], scalar=rstd[:rows, 0:1], in1=g_qa[:rows],
                            op0=ALU.mult, op1=ALU.mult), [r_qf, r_st, r_const], [r_qbf])
                        p, rp = nxt_pT()
                        transposes(qbf, rows, QL // 128, p, rp,
                                   lambda j0, n, rows=rows: cqT[:, j0:j0 + n, :rows], r_qbf, r_cqT)
                        gemm_tm(cqT, r_cqT, QL // 128, [(0, rows)], I["w_qb"], 0, NQB, [pacc[2]], [r_pacc[2]],
                                lambda _t, b0, w, acc, r_acc, rows=rows: act_copy(qf[:rows, b0:b0 + w], acc[:rows, :w], [r_acc], [r_qf]))
                        tr.dma("sp", tabB[:rows, 0:HB * 32], I["cosB"][pos0:pos0 + rows, :], writes=[r_tabB], lane_res=r_tabB)
                        tr.dma("sp", tabB[:rows, HB * 32:], I["sinB"][pos0:pos0 + rows, :], writes=[r_tabB], lane_res=r_tabB)
                        rope(qf, rows, HB, 192, 32, tabB[:, 0:HB * 32], tabB[:, HB * 32:], rtmp, r_qf, r_tabB, r_rtmp, off=128)
                        act_copy(qbf[:rows, :NQB], qf[:rows, :NQB], [r_qf], [r_qbf])
                        p, rp = nxt_pT()
                        transposes(qbf, rows, HB, p, rp,
                                   lambda j0, n, t0=t0, rows=rows: stN[:, j0:j0 + n, t0:t0 + rows], r_qbf, r_stN, stride=192)
                        p, rp = nxt_pT()
                        transposes(qbf, rows, HB, p, rp,
                                   lambda j0, n, t0=t0, rows=rows: stP[:, j0:j0 + n, t0:t0 + rows], r_qbf, r_stP,
                                   chw=64, stride=192, off=128)
                    if cfg.get("p2cut", 9) < 3:
                        return
                    tr.dma("sp", S["QT"][:, col0:col0 + ntok].rearrange("(m p) t -> p m t", p=128), stQ[:, :, :ntok],
                           reads=[r_stQ], lane_res=r_stQ)
                    tr.dma("sp", S["QBN"][:, col0:col0 + ntok].rearrange("(m p) t -> p m t", p=128), stN[:, :, :ntok],
                           reads=[r_stN], lane_res=r_stN)
                    tr.dma("sp", S["QBP"][:, col0:col0 + ntok].rearrange("(m p) t -> p m t", p=64), stP[:, :, :ntok],
                           reads=[r_stP], lane_res=r_stP)
                for grp in QGROUPS:
                    do_group2(grp)
                tr.barrier()

        phase2()
        def phase3():
            if PHASES >= 3:
              with ExitStack() as st3_:
                st3 = stack if cfg.get('noalias') else st3_
                NKP = SEQ
                NKS_ = NKTS * 128
                visb = sb("visb", [128, NKT], F32, st3)
                lamt = sb("lamt", [128, 256], F32, st3)
                g_sub = sb("g_sub", [128, 128], F32, st3)
                zero_c = sb("zero_c", [128, 1], F32, st3)
                for t_, n_ in ((visb, "visb"), (lamt, "lam"), (g_sub, "g_sub")):
                    tr.dma("sp", t_[:], I[n_][:], writes=[r_const], lane_res=r_const)
                lw = sb("lw", [128, 64], F32, st3); lam_e = sb("lam_e", [128, 4], F32, st3); r_lam = Res("lam")
                tr.op("dve", lambda e: e.memset(zero_c[:], 0.0), [], [r_const])
                tr.op("dve", lambda e: e.memset(lam_e[:], 0.0), [], [r_lam])
                for m in range(2):
                    tr.op("dve", lambda e, m=m: e.tensor_tensor(out=lw[:], in0=lamt[:, 128 * m:128 * m + 64],
                                                                in1=lamt[:, 128 * m + 64:128 * m + 128], op=ALU.mult),
                          [r_const], [r_lam])
                    tr.op("dve", lambda e, m=m: e.tensor_reduce(out=lam_e[:, m:m + 1], in_=lw[:], axis=AX.X, op=ALU.add),
                          [r_lam], [r_lam])
                tr.op("act", lambda e: e.activation(out=lam_e[:, 0:2], in_=lam_e[:, 0:2], func=AF.Exp), [r_lam], [r_lam])
                tr.op("dve", lambda e: e.tensor_tensor(out=lam_e[:, 2:3], in0=lam_e[:, 1:2], in1=lam_e[:, 0:1], op=ALU.subtract),
                      [r_lam], [r_lam])
                tr.op("dve", lambda e: e.tensor_scalar(out=lam_e[:, 3:4], in0=lam_e[:, 2:3], scalar1=-LAM_INIT, scalar2=None,
                                                       op0=ALU.add), [r_lam], [r_lam])
                tr.op("dve", lambda e: e.tensor_scalar(out=g_sub[:], in0=g_sub[:], scalar1=1.0 - LAM_INIT, scalar2=None,
                                                       op0=ALU.mult), [r_const], [r_const])
                kA = sb("kA", [128, NKP], BF16, st3); r_kA = Res("kA")
                vA = sb("vA", [128, NKT, 136], BF16, st3); r_vA = Res("vA")
                qA = sb("qA", [128, 2, NQ], BF16, st3); r_qA = Res("qA")
                kB = sb("kB", [128, NKP], BF16, st3); r_kB = Res("kB")
                kP = sb("kP", [128, NKP], BF16, st3); r_kP = Res("kP")
                kPs = sb("kPs", [128, NKS_], BF16, st3); r_kPs = Res("kPs")
                qBn = sb("qBn", [128, NQ], BF16, st3); r_qBn = Res("qBn")
                qBp = sb("qBp", [128, NQ], BF16, st3); r_qBp = Res("qBp")
                NPB = 3
                et = [sb(f"et{i}", [128, 512], BF16, st3) for i in range(NPB)]; r_et = [Res(f"et{i}") for i in range(NPB)]
                o1 = sb("o1", [128, 128], F32, st3); o2 = sb("o2", [128, 128], F32, st3); r_o = Res("o12")
                rr = sb("rr", [128, 4], F32, st3)
                ob = [sb(f"ob{i}", [128, 128], BF16, st3) for i in range(2)]; r_ob = [Res(f"ob{i}") for i in range(2)]
                ssa = sb("ssa", [128, 1], F32, st3); rsa = sb("rsa", [128, 1], F32, st3); r_sa = Res("sa")
                jnk = sb("jnk3", [128, 128], BF16, st3); r_jnk = Res("jnk3")
                pst = [pacc[0], pacc[1], ps("pst2", [128, 512], F32, st3)]; r_pst = [r_pacc[0], r_pacc[1], Res("pst2")]
                pab = [pacc[2], pacc[3], ps("pab2", [128, 512], F32, st3)]
                r_pab = [Res(f"pab{i}") for i in range(3)]
                ACC = [(pab[i // 3][:, (i % 3) * 129:(i % 3 + 1) * 129], r_pab[i // 3]) for i in range(8)]
                tr.op("dve", lambda e: e.memset(vA[:, :, 128:129], 1.0), [], [r_vA])
                tr.op("dve", lambda e: e.memset(qA[:], 0.0), [], [r_qA])
                tr.op("dve", lambda e: e.memset(kP[64:128, :], 0.0), [], [r_kP])
                tr.op("dve", lambda e: e.memset(kPs[64:128, :], 0.0), [], [r_kPs])
                tr.op("dve", lambda e: e.memset(qBp[64:128, :], 0.0), [], [r_qBp])
                tr.dma("sp", kP[0:64, :], S["KPET_p"][:, :], writes=[r_kP], lane_res=r_kP)
                tr.dma("sp", kPs[0:64, :], S["KPET_s"][:, :], writes=[r_kPs], lane_res=r_kPs)
                st_rr = [0]
                ob_rr = [0]

                def attend(parts, r_parts, nkt, k_last_rows, q_tiles, accs, scale, prompt):
                    c_lo = q_tiles[0][0]
                    c_hi = q_tiles[-1][0] + q_tiles[-1][1]
                    for (acc, r_acc) in accs:
                        tr.op("dve", lambda e, acc=acc: e.memset(acc, 0.0), [], [r_acc])
                    pending = []
                    for i in range(nkt):
                        krows = k_last_rows if i == nkt - 1 else 128
                        if prompt and i < NQT:
                            qt = [(k_, t) for k_, t in enumerate(q_tiles) if t[2] >= i]
                        else:
                            qt = list(enumerate(q_tiles))
                        if not qt:
                            continue
                        a = qt[0][1][0]
                        n = c_hi - a
                        si = st_rr[0] % NPB
                        st_rr[0] += 1

                        def f(e, i=i, krows=krows, a=a, n=n, si=si):
                            ins = None
                            for pi, (k_ap, q_ap) in enumerate(parts):
                                ins = e.matmul(pst[si][:krows, :n], lhsT=k_ap[:, i * 128:i * 128 + krows], rhs=q_ap[:, a:a + n],
                                               start=(pi == 0), stop=(pi == len(parts) - 1))
                            return ins
                        tr.op("pe", f, r_parts, [r_pst[si]])
                        bias = visb[:krows, i:i + 1] if prompt else zero_c[:krows, 0:1]
                        tr.op("act", lambda e, krows=krows, n=n, si=si, bias=bias: e.activation(
                            out=et[si][:krows, :n], in_=pst[si][:krows, :n], func=AF.Exp, bias=bias, scale=scale),
                            [r_pst[si], r_const], [r_et[si]])
                        if prompt and i < NQT:
                            for k_, t in qt:
                                if t[2] == i:
                                    tr.op("dve", lambda e, c=t[0] - a, si=si: e.memset(et[si][64:128, c:c + 64], 0.0),
                                          [], [r_et[si]])

                        def pv(qt=qt, a=a, si=si, i=i, krows=krows):
                            def g(e):
                                ins = None
                                for k_, t in qt:
                                    ins = e.matmul(accs[k_][0][:t[1], :], lhsT=et[si][:krows, t[0] - a:t[0] - a + t[1]],
                                                   rhs=r_vcur[0][:krows, i, 0:129], start=False, stop=False, skip_group_check=True)
                                return ins
                            wr = list({id(accs[k_][1]): accs[k_][1] for k_, t in qt}.values())
                            tr.op("pe", g, [r_et[si], r_vcur[1]], wr)
                        pending.append(pv)
                        if len(pending) > NPB - 1:
                            pending.pop(0)()
                    for pv_ in pending:
                        pv_()

                r_vcur = [None, None]
                PCH = [[(j * 128, 128, j) for j in range(c0, min(c0 + 4, NQT))] for c0 in range(0, NQT, 4)]
                SCH = [[(NPQ, 64, 0)]]

                def store_o(dst, h, t, src_fn):
                    i = ob_rr[0] % 2
                    ob_rr[0] += 1
                    src_fn(ob[i], r_ob[i])
                    tr.dma("sp", dst[t[0]:t[0] + t[1], h * 128:(h + 1) * 128], ob[i][:t[1], :], reads=[r_ob[i]], lane_res=r_ob[i])

                for h in range(HA):
                    tr.dma("sp", qA[0:64, 0, :], S["QT"][h * 64:(h + 1) * 64, :], writes=[r_qA], lane_res=r_qA)
                    tr.dma("sp", qA[64:128, 1, :], S["QT"][(HA + h) * 64:(HA + h + 1) * 64, :], writes=[r_qA], lane_res=r_qA)
                    for (prompt, sc, nk, nkt, klast, chunks) in ((True, "p", NKP, NKT, 128, PCH),
                                                              (False, "s", NKS_, NKTS, TKS - (NKTS - 1) * 128, SCH)):
                        tr.dma("sp", kA[0:64, :nk], S["KT_" + sc][h * 64:(h + 1) * 64, :], writes=[r_kA], lane_res=r_kA)
                        tr.dma("sp", kA[64:128, :nk], S["KT_" + sc][(HA + h) * 64:(HA + h + 1) * 64, :], writes=[r_kA], lane_res=r_kA)
                        tr.dma("sp", vA[:, :nkt, 0:128], S["V_" + sc][:, h * 128:(h + 1) * 128].rearrange("(i p) d -> p i d", p=128),
                               writes=[r_vA], lane_res=r_vA)
                        r_vcur[0], r_vcur[1] = vA, r_vA
                        for ch in chunks:
                            a1 = ACC[0:len(ch)]
                            a2 = ACC[4:4 + len(ch)]
                            attend([(kA[:, :], qA[:, 0, :])], [r_kA, r_qA], nkt, klast, ch, a1, SC_A, prompt)
                            attend([(kA[:, :], qA[:, 1, :])], [r_kA, r_qA], nkt, klast, ch, a2, SC_A, prompt)
                            for k_, t in enumerate(ch):
                                rows = t[1]
                                (c1, r1), (c2, r2) = a1[k_], a2[k_]
                                tr.op("dve", lambda e, c1=c1, rows=rows: e.reciprocal(out=rr[:rows, 0:1], in_=c1[:rows, 128:129]), [r1], [r_o])
                                tr.op("dve", lambda e, c2=c2, rows=rows: e.reciprocal(out=rr[:rows, 1:2], in_=c2[:rows, 128:129]), [r2], [r_o])
                                tr.op("dve", lambda e, rows=rows: e.tensor_tensor(out=rr[:rows, 2:3], in0=rr[:rows, 1:2], in1=lam_e[:rows, 3:4],
                                                                                  op=ALU.mult), [r_lam], [r_o])
                                tr.op("dve", lambda e, c1=c1, rows=rows: e.tensor_scalar(out=o1[:rows, :], in0=c1[:rows, 0:128], scalar1=rr[:rows, 0:1],
                                                                                       scalar2=None, op0=ALU.mult), [r1], [r_o])
                                tr.op("dve", lambda e, c2=c2, rows=rows: e.scalar_tensor_tensor(out=o2[:rows, :], in0=c2[:rows, 0:128], scalar=rr[:rows, 2:3],
                                                                                              in1=o1[:rows, :], op0=ALU.mult, op1=ALU.add), [r2], [r_o])
                                rstd_of(o2[:rows, :], rows, 128, jnk, r_jnk, ssa, rsa, r_o, r_sa)
                                store_o(S["OA"], h, t, lambda o_, r_o_, rows=rows: tr.op("dve", lambda e: e.scalar_tensor_tensor(
                                    out=o_[:rows, :], in0=o2[:rows, :], scalar=rsa[:rows, 0:1], in1=g_sub[:rows, :],
                                    op0=ALU.mult, op1=ALU.mult), [r_o, r_sa, r_const], [r_o_]))
                for h in range(HB):
                    tr.dma("sp", qBn[:, :], S["QBN"][h * 128:(h + 1) * 128, :], writes=[r_qBn], lane_res=r_qBn)
                    tr.dma("sp", qBp[0:64, :], S["QBP"][h * 64:(h + 1) * 64, :], writes=[r_qBp], lane_res=r_qBp)
                    for (prompt, sc, nk, nkt, klast, chunks, kp_, r_kp_) in ((True, "p", NKP, NKT, 128, PCH, kP, r_kP),
                                                                          (False, "s", NKS_, NKTS, TKS - (NKTS - 1) * 128, SCH, kPs, r_kPs)):
                        tr.dma("sp", kB[:, :nk], S["KBT_" + sc][h * 128:(h + 1) * 128, :], writes=[r_kB], lane_res=r_kB)
                        tr.dma("sp", vA[:, :nkt, 0:128], S["VB_" + sc][:, h * 128:(h + 1) * 128].rearrange("(i p) d -> p i d", p=128),
                               writes=[r_vA], lane_res=r_vA)
                        r_vcur[0], r_vcur[1] = vA, r_vA
                        for ch in chunks:
                            a1 = ACC[0:len(ch)]
                            attend([(kB[:, :], qBn[:, :]), (kp_[:, :], qBp[:, :])], [r_kB, r_qBn, r_kp_, r_qBp], nkt, klast, ch, a1, SC_B, prompt)
                            for k_, t in enumerate(ch):
                                rows = t[1]
                                c1, r1 = a1[k_]
                                tr.op("dve", lambda e, c1=c1, rows=rows: e.reciprocal(out=rr[:rows, 0:1], in_=c1[:rows, 128:129]), [r1], [r_o])
                                store_o(S["OB"], h, t, lambda o_, r_o_, rows=rows, c1=c1, r1=r1: tr.op("dve", lambda e: e.tensor_scalar(
                                    out=o_[:rows, :], in0=c1[:rows, 0:128], scalar1=rr[:rows, 0:1], scalar2=None, op0=ALU.mult),
                                    [r1, r_o], [r_o_]))
                tr.barrier()

        phase3()
        def phase4():
            if PHASES >= 4:
              Hs = dscr("Hs", [NQ, D], F32)
              Ys = dscr("Ys", [NQ, D], F32)
              with ExitStack() as st4:
                set_wslots(st4, 256)
                halves = [(wslots[i // 2][:, :, (i % 2) * 128:(i % 2) * 128 + 128], r_ws[i // 2][i % 2]) for i in range(2 * NWS)]
                hh = [0]

                def load_half(W, f):
                    i = hh[0] % len(halves)
                    hh[0] += 1
                    ap, r = halves[i]
                    src = W[:, f * 128:(f + 1) * 128].rearrange("(k p) c -> p k c", p=128)
                    cached_load((id(W), "half", f), ap[:, :KC, :], src, [128, KC, 128], [r], r)
                    return ap, r
                gbc = sb("gbc", [128, D], F32, st4); r_gbc = Res("gbc")
                wcv = sb("wcv", [128, 3 * NF], F32, st4); bcv = sb("bcv", [128, NF], F32, st4)
                stc = sb("stc", [128, 2 * NF], F32, st4); flag = sb("flag", [128, 1], F32, st4)
                identf = sb("identf", [128, 128], F32, st4)
                for t_, n_ in ((wcv, "w_conv"), (bcv, "b_conv"), (stc, "st_conv"), (flag, "flag"), (identf, "identf")):
                    tr.dma("sp", t_[:], I[n_][:], writes=[r_const], lane_res=r_const)
                tr.dma("sp", gbc[:], I["g_ffn"][:], writes=[r_gbc], lane_res=r_gbc)
                htile = sb("htile", [128, D], F32, st4); r_h = Res("htile")
                xs_bf = sb("xs_bf4", [128, max(D, NVA, NVB)], BF16, st4); r_xs = Res("xs_bf4")
                ss = sb("ss_4", [128, 1], F32, st4); rstd = sb("rstd_4", [128, 1], F32, st4); r_st = Res("stat_4")
                big = sb("big", [128, max(NF, KC) * 512], BF16, st4); r_big = Res("big")
                mT = big[:, :KC * 512].rearrange("p (k t) -> p k t", t=512)
                actT = big[:, :NF * 512].rearrange("p (k t) -> p k t", t=512)
                hnT = sb("hnT", [128, max(KC, NVA // 128, NVB // 128), 512], BF16, st4); r_hnT = Res("hnT")
                gb = sb("gb", [128, 512], BF16, st4); r_gb = Res("gb")
                ugP = sb("ugP", [128, 2 + 512], F32, st4); ugS = sb("ugS", [128, 2 + 64], F32, st4); r_ug = Res("ug")
                cb = sb("cb", [128, 512], F32, st4); r_cb = Res("cb")
                carry = sb("carry", [128, NF, 2], F32, st4); r_carry = Res("carry")
                cv = sb("cv", [128, 4, NF], F32, st4); r_cv = Res("cv")
                cvo = sb("cvo", [128, 128], F32, st4); r_cvo = Res("cvo")
                hblk = sb("hblk", [128, WB], F32, st4); r_hblk = Res("hblk")
                yblk = sb("yblk", [128, WB], F32, st4); r_yblk = Res("yblk")
                r_Hs = Res("Hs")
                tr.op("dve", lambda e: e.memset(carry[:], 0.0), [], [r_carry])
                def do_group4(gi, grp):
                    tiles = [(i * 128, rows) for i, (_, rows, _, _) in enumerate(grp)]
                    ntok = tiles[-1][0] + tiles[-1][1]
                    col0 = grp[0][3]
                    has_s = grp[-1][1] == 64
                    npr = ntok - 64 if has_s else ntok
                    last = gi == len(QGROUPS) - 1
                    for (Osc, Wb, nfe, first) in ((S["OA"], I["w_ba"], NVA, True), (S["OB"], I["w_bb"], NVB, False)):
                        for ti, (t0, rows) in enumerate(tiles):
                            tr.dma("sp", xs_bf[:rows, :nfe], Osc[col0 + t0:col0 + t0 + rows, :], writes=[r_xs], lane_res=r_xs)
                            p, rp = nxt_pT()
                            transposes(xs_bf, rows, nfe // 128, p, rp,
                                       lambda j0, n, t0=t0, rows=rows: hnT[:, j0:j0 + n, t0:t0 + rows], r_xs, r_hnT)

                        def ev_y(m, mw, acc, r_acc, first=first, ntok=ntok, col0=col0):
                            g_row = (0 if first else D) + m * 128
                            tr.dma("sp", gb[:mw, :ntok], S["GT"][g_row:g_row + mw, col0:col0 + ntok], writes=[r_gb], lane_res=r_gb)
                            if first:
                                tr.op("dve", lambda e: e.tensor_tensor(out=mT[:mw, m, :ntok], in0=acc[:mw, :ntok], in1=gb[:mw, :ntok],
                                                                       op=ALU.mult), [r_acc, r_gb], [r_big])
                            else:
                                tr.op("dve", lambda e: e.tensor_tensor(out=cb[:mw, :ntok], in0=acc[:mw, :ntok], in1=gb[:mw, :ntok],
                                                                       op=ALU.mult), [r_acc, r_gb], [r_cb])
                                tr.op("dve", lambda e: e.tensor_tensor(out=mT[:mw, m, :ntok], in0=cb[:mw, :ntok], in1=mT[:mw, m, :ntok],
                                                                       op=ALU.add), [r_cb], [r_big])
                        gemm_fm(hnT, r_hnT, nfe // 128, ntok, Wb, 0, D, pacc, r_pacc, ev_y)
                    for ti, (t0, rows) in enumerate(tiles):
                        tr.dma("sp", htile[:rows], grp[ti][0], writes=[r_h], lane_res=r_h)
                        gemm_tm(mT, r_big, KC, [(t0, rows)], I["w_out"], 0, D, [pacc[ti % 4]], [r_pacc[ti % 4]],
                                lambda _t, b0, w, acc, r_acc, rows=rows: tr.op("dve", lambda e: e.tensor_tensor(
                                    out=htile[:rows, b0:b0 + w], in0=acc[:rows, :w], in1=htile[:rows, b0:b0 + w], op=ALU.add),
                                    [r_acc], [r_h]))
                        tr.dma("sp", Hs[col0 + t0:col0 + t0 + rows, :], htile[:rows], reads=[r_h], writes=[r_Hs], lane_res=r_h)
                        rstd_of(htile[:rows, :], rows, D, xs_bf, r_xs, ss, rstd, r_h, r_st)
                        tr.op("dve", lambda e, rows=rows: e.scalar_tensor_tensor(
                            out=xs_bf[:rows, :D], in0=htile[:rows], scalar=rstd[:rows, 0:1], in1=gbc[:rows],
                            op0=ALU.mult, op1=ALU.mult), [r_h, r_st, r_gbc], [r_xs])
                        p, rp = nxt_pT()
                        transposes(xs_bf, rows, KC, p, rp,
                                   lambda j0, n, t0=t0, rows=rows: hnT[:, j0:j0 + n, t0:t0 + rows], r_xs, r_hnT)
                    for f in range(NF):
                        if True:
                            wg, r_wg = load_half(I["w_fg"], f)
                            wu, r_wu = load_half(I["w_fu"], f)
                            ag, r_ag = pacc[(f % 2)], r_pacc[(f % 2)]
                            au, r_au = pacc[2 + (f % 2)], r_pacc[2 + (f % 2)]
                            for (wt, r_w, acc, r_acc) in ((wg, r_wg, ag, r_ag), (wu, r_wu, au, r_au)):
                                def fmm(e, wt=wt, acc=acc):
                                    ins = None
                                    for k in range(KC):
                                        ins = e.matmul(acc[:, :ntok], lhsT=wt[:, k, :], rhs=hnT[:, k, :ntok],
                                                       start=(k == 0), stop=(k == KC - 1))
                                    return ins
                                tr.op("pe", fmm, [r_hnT, r_w], [r_acc])
                            if npr > 0:
                                tr.op("act", lambda e, ag=ag: e.activation(out=ugP[:, 2:2 + npr], in_=ag[:, :npr], func=AF.Copy),
                                      [r_ag], [r_ug])
                                tr.op("dve", lambda e, f=f: e.tensor_copy(out=ugP[:, 0:2], in_=carry[:, f, :]), [r_carry], [r_ug])
                                if gi == 0:
                                    tr.op("dve", lambda e: e.tensor_scalar(out=ugP[:, 128:130], in0=ugP[:, 128:130], scalar1=flag[:, 0:1],
                                                                           scalar2=None, op0=ALU.mult), [r_const], [r_ug])
                                tr.op("dve", lambda e, f=f: e.tensor_copy(out=carry[:, f, :], in_=ugP[:, npr:npr + 2]), [r_ug], [r_carry])
                                if last:
                                    tr.op("dve", lambda e, f=f: e.tensor_copy(out=cv[:, 0:2, f], in_=ugP[:, npr:npr + 2]), [r_ug], [r_cv])
                            if has_s:
                                tr.op("act", lambda e, ag=ag: e.activation(out=ugS[:, 2:66], in_=ag[:, npr:npr + 64], func=AF.Copy),
                                      [r_ag], [r_ug])
                                tr.op("dve", lambda e, f=f: e.tensor_copy(out=ugS[:, 0:2], in_=stc[:, 2 * f:2 * f + 2]), [r_const], [r_ug])
                                tr.op("dve", lambda e, f=f: e.tensor_copy(out=cv[:, 2:4, f], in_=ugS[:, 64:66]), [r_ug], [r_cv])
                            for (ug, c0_, n_) in ((ugP, 0, npr), (ugS, npr, 64 if has_s else 0)):
                                if n_ == 0:
                                    continue
                                tr.op("act", lambda e, ug=ug, c0_=c0_, n_=n_, f=f: e.activation(
                                    out=cb[:, c0_:c0_ + n_], in_=ug[:, 2:2 + n_], func=AF.Identity,
                                    bias=bcv[:, f:f + 1], scale=wcv[:, 2 * NF + f:2 * NF + f + 1]), [r_ug, r_const], [r_cb])
                                for j in (1, 0):
                                    tr.op("dve", lambda e, ug=ug, c0_=c0_, n_=n_, f=f, j=j: e.scalar_tensor_tensor(
                                        out=cb[:, c0_:c0_ + n_], in0=ug[:, j:j + n_], scalar=wcv[:, j * NF + f:j * NF + f + 1],
                                        in1=cb[:, c0_:c0_ + n_], op0=ALU.mult, op1=ALU.add), [r_ug, r_const], [r_cb])
                            tr.op("act", lambda e: e.activation(out=cb[:, :ntok], in_=cb[:, :ntok], func=AF.Silu), [r_cb], [r_cb])
                            tr.op("dve", lambda e, f=f, au=au: e.tensor_tensor(out=actT[:, f, :ntok], in0=cb[:, :ntok], in1=au[:, :ntok],
                                                                             op=ALU.mult), [r_cb, r_au], [r_big])
                    def ev_d(ti, b0, w, acc, r_acc, tiles=tiles, col0=col0):
                        t0, rows = tiles[ti]
                        tr.dma("sp", hblk[:rows, :w], Hs[col0 + t0:col0 + t0 + rows, b0:b0 + w], reads=[r_Hs], writes=[r_hblk], lane_res=r_hblk)
                        tr.op("dve", lambda e: e.tensor_tensor(out=yblk[:rows, :w], in0=acc[:rows, :w], in1=hblk[:rows, :w], op=ALU.add),
                              [r_acc, r_hblk], [r_yblk])
                        tr.dma("sp", Ys[col0 + t0:col0 + t0 + rows, b0:b0 + w], yblk[:rows, :w], reads=[r_yblk], lane_res=r_yblk)
                    gemm_tm(actT, r_big, NF, tiles, I["w_fd"], 0, D, pacc, r_pacc, ev_d)
                for gi, grp in enumerate(QGROUPS):
                    do_group4(gi, grp)
                for j, dst in ((0, O["conv_p"][0]), (1, O["conv_p"][1]), (2, O["conv_s"][0]), (3, O["conv_s"][1])):
                    if cfg.get("p4cut", 9) < 5:
                        break
                    dv_ = dst.rearrange("(f p) -> p f", p=128)
                    for f0 in range(0, NF, 8):
                        f1 = min(NF, f0 + 8)
                        ln = tr.lane_of(r_cv)
                        tr.op("sp", lambda e, o=dv_[:, f0:f1], i=cv[:, j, f0:f1]: e.dma_start(
                            out=o, in_=i, allow_slow_non_contiguous=True), [r_cv], [], lane=ln)
                tr.barrier()
                tr.dma("sp", gbc[:], I["g_final"][:], writes=[r_gbc], lane_res=r_gbc)
                for j, (src, rows, _, col) in enumerate(QTILES):
                    if j == 0:
                        continue
                    dst = O["y_s"][:, :] if rows == 64 else O["y_p"][(j - 1) * 128:j * 128, :]
                    tr.dma("sp", htile[:rows], Ys[col:col + rows, :], writes=[r_h], lane_res=r_h)
                    rstd_of(htile[:rows, :], rows, D, xs_bf, r_xs, ss, rstd, r_h, r_st)
                    tr.op("dve", lambda e, rows=rows: e.scalar_tensor_tensor(
                        out=htile[:rows], in0=htile[:rows], scalar=rstd[:rows, 0:1], in1=gbc[:rows],
                        op0=ALU.mult, op1=ALU.mult), [r_st, r_gbc], [r_h])
                    tr.dma("sp", dst, htile[:rows], reads=[r_h], lane_res=r_h)

        phase4()
        tr.barrier()
        with nc.Block() as block:
            tr.emit(block)
    return nc


def build_rest(L):
    raise NotImplementedError


def rope_tables(pos, half):
    inv = np.power(np.float32(ROPE_THETA), -np.arange(half, dtype=np.float32) / np.float32(half)).astype(np.float32)
    ang = pos.astype(np.float32)[:, None] * inv[None, :]
    return np.cos(ang).astype(np.float32), np.sin(ang).astype(np.float32)


def make_in_maps(cfg, inp):
    D, SEQ, PAST = cfg["D"], cfg["SEQ"], cfg["PAST"]
    HA, HB, QL, KVL, FF = cfg["HA"], cfg["HB"], cfg["QL"], cfg["KVL"], cfg["FF"]
    TB = SEQ // 8
    NQT = TB // 128 + 1
    NKT = SEQ // 128
    KC = D // 128
    NF = FF // 128
    f = lambda a: np.ascontiguousarray(np.asarray(a, dtype=np.float32))
    bc = lambda v: np.ascontiguousarray(np.broadcast_to(f(v).reshape(1, -1), (128, f(v).size)))
    xp = f(inp["x_prompt"])[0]
    w_kvb = f(inp["w_kvb"])[0].reshape(KVL, HB, 256)
    shared = dict(
        w_in=f(inp["w_in"])[0], w_qb=f(inp["w_qb"])[0],
        w_kvn=np.ascontiguousarray(w_kvb[:, :, :128].reshape(KVL, HB * 128)),
        w_kvv=np.ascontiguousarray(w_kvb[:, :, 128:].reshape(KVL, HB * 128)),
        w_ba=f(inp["w_branch_a"])[0], w_bb=f(inp["w_branch_b"])[0], w_out=f(inp["w_out"])[0],
        w_fg=f(inp["w_ff_gate"])[0], w_fu=f(inp["w_ff_up"])[0], w_fd=f(inp["w_ff_down"])[0],
        g_attn=bc(inp["g_attn"][0]), g_ffn=bc(inp["g_ffn"][0]), g_final=bc(inp["g_final"]),
        g_qa=bc(inp["g_qa"][0]), g_kva=bc(inp["g_kva"][0]), g_sub=bc(inp["g_subln"][0]),
        b_gate=np.ascontiguousarray(f(inp["b_gate"])[0].reshape(2 * KC, 128).T),
        w_conv=np.ascontiguousarray(f(inp["w_conv"])[0].reshape(3, NF, 128).transpose(2, 0, 1).reshape(128, 3 * NF)),
        b_conv=np.ascontiguousarray(f(inp["b_conv"])[0].reshape(NF, 128).T),
        lam=np.ascontiguousarray(np.broadcast_to(np.concatenate(
            [f(inp[k])[0] for k in ("lam_q1", "lam_k1", "lam_q2", "lam_k2")]).reshape(1, 256), (128, 256))),
        ident=np.eye(128, dtype=np.float32).astype(ml_dtypes.bfloat16), identf=np.eye(128, dtype=np.float32),
    )
    maps = []
    for c in range(8):
        g0 = TB * c - 128
        pos = (g0 + np.arange(SEQ)) % SEQ
        pos_all = np.concatenate([pos, PAST + np.arange(64)])
        cA, sA = rope_tables(pos_all, 8)
        cB, sB = rope_tables(pos_all, 32)
        visb = np.zeros((128, NKT), np.float32)
        for i in range(NKT):
            if i == 0:
                vis = c > 0
            elif i < NQT:
                vis = True
            else:
                vis = (g0 + 128 * i) >= SEQ
            visb[:, i] = 0.0 if vis else NEGB
        m = dict(shared)
        m.update(
            xk=np.ascontiguousarray(xp[pos]), xs=f(inp["x_sample"])[c],
            c_dk=f(inp["cache_dk"])[0, c].reshape(PAST, -1), c_dv=f(inp["cache_dv"])[0, c].reshape(PAST, -1),
            c_ckv=f(inp["cache_ckv"])[0, c], c_kpe=f(inp["cache_kpe"])[0, c],
            st_conv=np.ascontiguousarray(f(inp["state_conv"])[0, c].reshape(2, NF, 128).transpose(2, 1, 0).reshape(128, 2 * NF)),
            cosA=np.ascontiguousarray(np.tile(cA, (1, 2 * HA))), sinA=np.ascontiguousarray(np.tile(sA, (1, 2 * HA))),
            cosB=np.ascontiguousarray(np.tile(cB, (1, HB))), sinB=np.ascontiguousarray(np.tile(sB, (1, HB))),
            visb=visb, flag=np.full((128, 1), 0.0 if c == 0 else 1.0, np.float32),
        )
        maps.append(m)
    return maps


def assemble(cfg, res):
    D, SEQ = cfg["D"], cfg["SEQ"]
    HA, HB, KVL, FF = cfg["HA"], cfg["HB"], cfg["KVL"], cfg["FF"]
    R = res
    cat = lambda k: np.concatenate([R[c][k] for c in range(8)], axis=0)
    stk = lambda k: np.stack([R[c][k] for c in range(8)], axis=0)
    return (
        cat("y_p")[None], stk("y_s"),
        cat("dk_p").reshape(1, 1, SEQ, 2 * HA, 64), cat("dv_p").reshape(1, 1, SEQ, HA, 128),
        cat("ckv_p").reshape(1, 1, SEQ, KVL), cat("kpe_p").reshape(1, 1, SEQ, 64),
        R[7]["conv_p"].reshape(1, 1, 2, FF),
        stk("dk_s").reshape(1, 8, 64, 2 * HA, 64), stk("dv_s").reshape(1, 8, 64, HA, 128),
        stk("ckv_s").reshape(1, 8, 64, KVL), stk("kpe_s").reshape(1, 8, 64, 64),
        stk("conv_s").reshape(1, 8, 2, FF),
    )


def run(cfg, inp):
    nc = build(cfg)
    maps = make_in_maps(cfg, inp)
    res = run_bass_kernel_spmd(nc, maps, core_ids=list(range(8)))
    return assemble(cfg, res.results)


def kernel(**inputs):
    outs = run(FULL, inputs)
    return tuple(np.ascontiguousarray(o, dtype=np.float32) for o in outs)
```

```python
import math
from contextlib import ExitStack
import numpy as np
import ml_dtypes
import concourse.bass as bass
import concourse.mybir as mybir
from concourse.bass_utils import run_bass_kernel_spmd

F32 = mybir.dt.float32
BF16 = mybir.dt.bfloat16
AF = mybir.ActivationFunctionType
ALU = mybir.AluOpType
AX = mybir.AxisListType
NS = 8
EPS = 1e-6
NEGB = -60.0
SCR_KIND = "Internal"
ROPE_THETA = 500000.0

FULL = dict(D=4096, SEQ=8192, PAST=2048, HA=16, HB=16, QL=896, KVL=512, FF=11008)


class Res:
    __slots__ = ("name", "w", "rd", "lane")

    def __init__(self, name):
        self.name = name
        self.w = None
        self.rd = {}
        self.lane = None


class Lane:
    __slots__ = ("key", "sem", "k")


class TR:
    def __init__(self, nc, stack, n_lanes=None):
        n_lanes = len(nc.free_semaphores) - 3 * NS - 2
        self.nc = nc
        self.engs = {}
        for e in ("pe", "act", "dve", "pool", "sp"):
            sems = [stack.enter_context(nc.semaphore(f"s_{e}_{i}")) for i in range(NS)] if e in ("pe", "act", "dve") else []
            self.engs[e] = dict(sems=sems, n=0, ops=[], known={})
        self.free_lanes = [stack.enter_context(nc.semaphore(f"s_l{i}")) for i in range(n_lanes)]
        self.lanes = []

    def lane_of(self, res):
        if res.lane is None:
            ln = Lane()
            ln.key = "L%d" % len(self.lanes)
            ln.sem = self.free_lanes.pop()
            ln.k = 0
            self.lanes.append(ln)
            res.lane = ln
        return res.lane

    def op(self, eng, fn, reads=(), writes=(), lane=None):
        E = self.engs[eng]
        waits = {}

        def need(ev):
            if ev is None:
                return
            key, idx = ev[0], ev[1]
            if key == "pe" and eng == "pe":
                return
            if E["known"].get(key, -1) >= idx:
                return
            if key not in waits or waits[key][1] < idx:
                waits[key] = ev

        for r in reads:
            need(r.w)
        for w in writes:
            need(w.w)
            for ev in w.rd.values():
                need(ev)
        for key, ev in waits.items():
            E["known"][key] = ev[1]
        if fn is None:
            E["ops"].append((list(waits.values()), None, None, 0))
            return
        if lane is None:
            n = E["n"]
            E["n"] += 1
            ev = (eng, n, E["sems"][n % NS], n // NS + 1)
            inc = 1
        else:
            lane.k += 1
            ev = (lane.key, lane.k, lane.sem, 16 * lane.k)
            inc = 16
        E["ops"].append((list(waits.values()), fn, ev[2], inc))
        for r in reads:
            r.rd[ev[0]] = ev
        for w in writes:
            w.w = ev
            w.rd = {}

    def dma(self, q, out, in_, reads=(), writes=(), lane_res=None):
        ln = self.lane_of(lane_res)
        self.op(q, lambda e, o=out, i=in_: e.dma_start(out=o, in_=i), reads, writes, lane=ln)

    def barrier(self):
        evs = []
        for name, E in self.engs.items():
            if E["n"] > 0:
                n = E["n"] - 1
                evs.append((name, n, E["sems"][n % NS], n // NS + 1))
        for ln in self.lanes:
            if ln.k > 0:
                evs.append((ln.key, ln.k, ln.sem, 16 * ln.k))
        for name, E in self.engs.items():
            waits = []
            for ev in evs:
                if E["known"].get(ev[0], -1) < ev[1]:
                    waits.append(ev)
                    E["known"][ev[0]] = ev[1]
            E["ops"].append((waits, None, None, 0))

    def emit(self, block):
        def mk(name):
            def f(e):
                for waits, fn, sem, inc in self.engs[name]["ops"]:
                    for ev in waits:
                        e.wait_ge(ev[2], ev[3])
                    if fn is not None:
                        fn(e).then_inc(sem, inc)
            return f
        block.tensor(mk("pe"))
        block.scalar(mk("act"))
        block.vector(mk("dve"))
        block.gpsimd(mk("pool"))
        block.sync(mk("sp"))


def build(cfg):
    D, SEQ, PAST = cfg["D"], cfg["SEQ"], cfg["PAST"]
    HA, HB, QL, KVL, FF = cfg["HA"], cfg["HB"], cfg["QL"], cfg["KVL"], cfg["FF"]
    TB = SEQ // 8
    NQT = TB // 128 + 1
    NKT = SEQ // 128
    NPQ = NQT * 128
    NQ = NPQ + 64
    KC = D // 128
    NKA, NVA, NVB, NKB = 2 * HA * 64, HA * 128, HB * 128, HB * 128
    NQB = HB * 192
    NF = FF // 128
    o_q, o_k, o_v = 0, NKA, 2 * NKA
    o_cq = 2 * NKA + NVA
    o_ckv = o_cq + QL
    o_g = o_ckv + KVL + 64
    N_IN = o_g + 2 * D
    TKS = PAST + 64
    NKTS = (TKS + 127) // 128
    SC_A = 64 ** -0.5
    SC_B = 192 ** -0.5
    LAM_INIT = 0.8 - 0.6 * math.exp(0.0)

    nc = bass.Bass("TRN2", target_bir_lowering=False)

    def din(name, shape, dt=F32):
        return nc.dram_tensor(name, list(shape), dt, kind="ExternalInput").ap()

    def dout(name, shape):
        return nc.dram_tensor(name, list(shape), F32, kind="ExternalOutput").ap()

    def dscr(name, shape, dt=BF16):
        return nc.dram_tensor(name, list(shape), dt, kind=SCR_KIND).ap()

    I = dict(
        xk=din("xk", [SEQ, D]), xs=din("xs", [64, D]),
        w_in=din("w_in", [D, N_IN]), w_qb=din("w_qb", [QL, NQB]),
        w_kvn=din("w_kvn", [KVL, NKB]), w_kvv=din("w_kvv", [KVL, NVB]),
        w_ba=din("w_ba", [NVA, D]), w_bb=din("w_bb", [NVB, D]), w_out=din("w_out", [D, D]),
        w_fg=din("w_fg", [D, FF]), w_fu=din("w_fu", [D, FF]), w_fd=din("w_fd", [FF, D]),
        g_attn=din("g_attn", [128, D]), g_ffn=din("g_ffn", [128, D]), g_final=din("g_final", [128, D]),
        g_qa=din("g_qa", [128, QL]), g_kva=din("g_kva", [128, KVL]), g_sub=din("g_sub", [128, 128]),
        b_gate=din("b_gate", [128, 2 * KC]), w_conv=din("w_conv", [128, 3 * NF]), b_conv=din("b_conv", [128, NF]),
        lam=din("lam", [128, 256]),
        c_dk=din("c_dk", [PAST, NKA]), c_dv=din("c_dv", [PAST, NVA]),
        c_ckv=din("c_ckv", [PAST, KVL]), c_kpe=din("c_kpe", [PAST, 64]), st_conv=din("st_conv", [128, 2 * NF]),
        cosA=din("cosA", [SEQ + 64, 2 * HA * 8]), sinA=din("sinA", [SEQ + 64, 2 * HA * 8]),
        cosB=din("cosB", [SEQ + 64, HB * 32]), sinB=din("sinB", [SEQ + 64, HB * 32]),
        visb=din("visb", [128, NKT]), flag=din("flag", [128, 1]),
        ident=din("ident", [128, 128], BF16), identf=din("identf", [128, 128]),
    )
    O = dict(
        y_p=dout("y_p", [TB, D]), y_s=dout("y_s", [64, D]),
        dk_p=dout("dk_p", [TB, NKA]), dv_p=dout("dv_p", [TB, NVA]),
        ckv_p=dout("ckv_p", [TB, KVL]), kpe_p=dout("kpe_p", [TB, 64]), conv_p=dout("conv_p", [2, FF]),
        dk_s=dout("dk_s", [64, NKA]), dv_s=dout("dv_s", [64, NVA]),
        ckv_s=dout("ckv_s", [64, KVL]), kpe_s=dout("kpe_s", [64, 64]), conv_s=dout("conv_s", [2, FF]),
    )
    S = dict(
        KT_p=dscr("KT_p", [NKA, SEQ]), V_p=dscr("V_p", [SEQ, NVA]),
        KBT_p=dscr("KBT_p", [NKB, SEQ]), KPET_p=dscr("KPET_p", [64, SEQ]), VB_p=dscr("VB_p", [SEQ, NVB]),
        KT_s=dscr("KT_s", [NKA, NKTS * 128]), V_s=dscr("V_s", [NKTS * 128, NVA]),
        KBT_s=dscr("KBT_s", [NKB, NKTS * 128]), KPET_s=dscr("KPET_s", [64, NKTS * 128]),
        VB_s=dscr("VB_s", [NKTS * 128, NVB]),
        QT=dscr("QT", [NKA, NQ]), QBN=dscr("QBN", [HB * 128, NQ]), QBP=dscr("QBP", [HB * 64, NQ]),
        GT=dscr("GT", [2 * D, NQ]), OA=dscr("OA", [NQ, NVA]), OB=dscr("OB", [NQ, NVB]),
    )

    with ExitStack() as stack:
        tr = TR(nc, stack)
        ec = stack.enter_context

        def sb(name, shape, dt=F32, st=None):
            return (st or stack).enter_context(nc.sbuf_tensor("sb_" + name, list(shape), dt))

        def ps(name, shape, dt=F32, st=None):
            return (st or stack).enter_context(nc.psum_tensor("ps_" + name, list(shape), dt))

        ident = sb("ident", [128, 128], BF16)
        r_const = Res("const")
        tr.dma("sp", ident[:], I["ident"][:], writes=[r_const], lane_res=r_const)

        def act_copy(out, in_, reads, writes, eng="act"):
            if eng == "act":
                tr.op("act", lambda e, o=out, i=in_: e.activation(out=o, in_=i, func=AF.Copy), reads, writes)
            else:
                tr.op("dve", lambda e, o=out, i=in_: e.tensor_copy(out=o, in_=i), reads, writes)

        cp_rr = [0]

        def any_copy(out, in_, reads, writes):
            cp_rr[0] ^= 1
            act_copy(out, in_, reads, writes, "act" if cp_rr[0] else "dve")

        def rstd_of(src, rows, n, junk, r_junk, ss, rstd, r_src, r_tmp):
            tr.op("dve", lambda e: e.memset(ss[:rows], 0.0), [], [r_tmp])
            tr.op("act", lambda e: e.activation(out=junk[:rows, :n], in_=src, func=AF.Square,
                                                accum_out=ss[:rows]), [r_src], [r_tmp, r_junk])
            tr.op("dve", lambda e: e.tensor_scalar(out=rstd[:rows], in0=ss[:rows], scalar1=1.0 / n, scalar2=EPS,
                                                   op0=ALU.mult, op1=ALU.add), [r_tmp], [r_tmp])
            tr.op("act", lambda e: e.activation(out=rstd[:rows], in_=rstd[:rows], func=AF.Sqrt), [r_tmp], [r_tmp])
            tr.op("dve", lambda e: e.reciprocal(out=rstd[:rows], in_=rstd[:rows]), [r_tmp], [r_tmp])

        def rope(buf, rows, nh, hd, half, cos, sin, tmp, r_buf, r_tab, r_tmp, off=0):
            v = buf[:rows, :nh * hd].rearrange("p (h d) -> p h d", d=hd)
            x1 = v[:, :, off:off + half]
            x2 = v[:, :, off + half:off + 2 * half]
            c = cos[:rows, :nh * half].rearrange("p (h d) -> p h d", d=half)
            s = sin[:rows, :nh * half].rearrange("p (h d) -> p h d", d=half)
            n = nh * half
            t = [tmp[:rows, i * n:(i + 1) * n].rearrange("p (h d) -> p h d", d=half) for i in range(4)]
            for (o, a, b) in ((t[0], x1, c), (t[1], x2, s), (t[2], x2, c), (t[3], x1, s)):
                tr.op("dve", lambda e, o=o, a=a, b=b: e.tensor_tensor(out=o, in0=a, in1=b, op=ALU.mult),
                      [r_buf, r_tab], [r_tmp])
            tr.op("dve", lambda e: e.tensor_tensor(out=x1, in0=t[0], in1=t[1], op=ALU.subtract), [r_tmp], [r_buf])
            tr.op("dve", lambda e: e.tensor_tensor(out=x2, in0=t[2], in1=t[3], op=ALU.add), [r_tmp], [r_buf])

        def transposes(src, rows, nch, pT, r_pT, dst_fn, r_src, r_dst, chw=128, stride=None, off=0):
            stride = stride or chw
            j0 = 0
            while j0 < nch:
                n = min(8, nch - j0)

                def f(e, j0=j0, n=n):
                    ins = None
                    for j in range(n):
                        ins = e.transpose(out=pT[:chw, j, :rows], in_=src[:rows, (j0 + j) * stride + off:(j0 + j) * stride + off + chw],
                                          identity=ident[:rows, :rows])
                    return ins
                tr.op("pe", f, [r_src, r_const], [r_pT])
                any_copy(dst_fn(j0, n), pT[:chw, 0:n, :rows], [r_pT], [r_dst])
                j0 += n

        NWS = 2
        WB = 256
        WBv = [WB]
        wslots = [None] * NWS
        r_ws = [[Res(f"wslot{i}a"), Res(f"wslot{i}b")] for i in range(NWS)]
        ws_gen = [0]

        def set_wslots(st, wb):
            ws_gen[0] += 1
            WBv[0] = wb
            for i in range(NWS):
                wslots[i] = sb(f"wslot{ws_gen[0]}_{i}", [128, 32, wb], BF16, st)
        ws_rr = [0]

        wcache = {}
        WCACHE = cfg.get("wcache", True)

        def cached_load(key, dst, src_fp32, shape, writes, lane_res):
            if key in wcache:
                scr, r_scr = wcache[key]
                tr.dma("pool", dst, scr, reads=[r_scr], writes=writes, lane_res=lane_res)
                return
            tr.dma("pool", dst, src_fp32, writes=writes, lane_res=lane_res)
            if WCACHE:
                scr = dscr("wc%d" % len(wcache), shape)
                r_scr = Res("wc%d" % len(wcache))
                tr.dma("sp", scr, dst, reads=writes, writes=[r_scr], lane_res=lane_res)
                wcache[key] = (scr, r_scr)

        def load_w(W, k0, nk, c0, w):
            i = ws_rr[0] % NWS
            ws_rr[0] += 1
            rows = W.shape[0]
            kk = min(128, rows - k0 * 128) if nk == 1 else 128
            src = W[k0 * 128:k0 * 128 + nk * kk, c0:c0 + w].rearrange("(k p) c -> p k c", p=kk)
            cached_load((id(W), k0, nk, c0, w), wslots[i][:kk, :nk, :w], src, [kk, nk, w], r_ws[i], r_ws[i][0])
            return wslots[i], r_ws[i]

        def gemm_tm(xT, r_x, kcn, tiles, W, c0, ncols, accs, r_accs, evac, wb=None):
            wb = wb or WBv[0]
            b0 = 0
            while b0 < ncols:
                w = min(wb, ncols - b0)
                nkb = (kcn + 31) // 32
                for kb in range(nkb):
                    k0 = kb * 32
                    nk = min(32, kcn - k0)
                    wt, r_w = load_w(W, k0, nk, c0 + b0, w)
                    for ti, (t0, rows) in enumerate(tiles):
                        acc, r_acc = accs[ti], r_accs[ti]

                        def f(e, wt=wt, t0=t0, rows=rows, k0=k0, nk=nk, w=w, acc=acc, kb=kb):
                            ins = None
                            for k in range(nk):
                                ins = e.matmul(acc[:rows, :w], lhsT=xT[:, k0 + k, t0:t0 + rows], rhs=wt[:, k, :w],
                                               start=(kb == 0 and k == 0), stop=(kb == nkb - 1 and k == nk - 1))
                            return ins
                        tr.op("pe", f, [r_x] + r_w, [r_acc])
                for ti, (t0, rows) in enumerate(tiles):
                    evac(ti, b0, w, accs[ti], r_accs[ti])
                b0 += w

        def gemm_fm(xT, r_x, kcn, ntok, W, c0, ncols, accs, r_accs, evac, kpart=128):
            b0 = 0
            mi = 0
            while b0 < ncols:
                w = min(WBv[0], ncols - b0)
                wt, r_w = load_w(W, 0, kcn, c0 + b0, w)
                for m0 in range(0, w, 128):
                    mw = min(128, w - m0)
                    acc, r_acc = accs[mi % len(accs)], r_accs[mi % len(accs)]

                    def f(e, wt=wt, m0=m0, mw=mw, acc=acc):
                        ins = None
                        for k in range(kcn):
                            ins = e.matmul(acc[:mw, :ntok], lhsT=wt[:kpart, k, m0:m0 + mw], rhs=xT[:kpart, k, :ntok],
                                           start=(k == 0), stop=(k == kcn - 1))
                        return ins
                    tr.op("pe", f, [r_x] + r_w, [r_acc])
                    evac((b0 + m0) // 128, mw, acc, r_acc)
                    mi += 1
                b0 += w

        pacc = [ps(f"pacc{i}", [128, 512], F32) for i in range(4)]; r_pacc = [Res(f"pacc{i}") for i in range(4)]
        pT = [ps(f"pT{i}", [128, 8, 128], BF16) for i in range(2)]; r_pT = [Res(f"pT{i}") for i in range(2)]
        pt_rr = [0]

        def nxt_pT():
            pt_rr[0] ^= 1
            return pT[pt_rr[0]], r_pT[pt_rr[0]]

        def phase1():
            with ExitStack() as st1_:
                st1 = stack if cfg.get('noalias') else st1_
                set_wslots(st1, cfg.get('wb1', 512))
                g_attn = sb("g_attn", [128, D], F32, st1)
                g_kva = sb("g_kva", [128, KVL], F32, st1)
                tr.dma("sp", g_attn[:], I["g_attn"][:], writes=[r_const], lane_res=r_const)
                tr.dma("sp", g_kva[:], I["g_kva"][:], writes=[r_const], lane_res=r_const)
                x_tm = sb("x_tm", [128, D], F32, st1); r_x = Res("x_tm")
                xs_bf = sb("xs_bf", [128, D], BF16, st1); r_xs = Res("xs_bf")
                xnT = sb("xnT", [128, KC, 512], BF16, st1); r_xnT = Res("xnT")
                ss = sb("ss", [128, 1], F32, st1); rstd = sb("rstd", [128, 1], F32, st1); r_st = Res("stat")
                ss2 = sb("ss2", [128, 1], F32, st1); rstd2 = sb("rstd2", [128, 1], F32, st1); r_st2 = Res("stat2")
                kf = [sb(f"kf{i}", [128, max(NKA, NVA, KVL + 64)], F32, st1) for i in range(4)]; r_kf = [Res(f"kf{i}") for i in range(4)]
                vf = kf; r_vf = r_kf
                cf = kf; r_cf = r_kf
                kbf = sb("kbf", [128, max(NKA, NVA, NVB)], BF16, st1); r_kbf = Res("kbf")
                vbf = kbf; r_vbf = r_kbf
                cbf = sb("cbf", [128, KVL + 64], BF16, st1); r_cbf = Res("cbf")
                tabA = sb("tabA", [128, 2 * 2 * HA * 8], F32, st1); r_tabA = Res("tabA")
                tabB = sb("tabB", [128, 2 * 32], F32, st1); r_tabB = Res("tabB")
                rtmp = sb("rtmp", [128, 4 * 2 * HA * 8], F32, st1); r_rtmp = Res("rtmp")
                stage = sb("stage", [128, max(NKA, NKB) // 128, 512], BF16, st1); r_stage = Res("stage")
                stage2 = stage; r_stage2 = r_stage
                ckvT = sb("ckvT", [128, KVL // 128, 512], BF16, st1); r_ckvT = Res("ckvT")
                kpeT = sb("kpeT", [64, 1, 512], BF16, st1); r_kpeT = Res("kpeT")
                vbb = kbf; r_vbb = r_kbf
                def make_xnT(tiles_src, g_bc):
                    for ti, (src, rows) in enumerate(tiles_src):
                        tr.dma("sp", x_tm[:rows], src, writes=[r_x], lane_res=r_x)
                        rstd_of(x_tm[:rows, :], rows, D, xs_bf, r_xs, ss, rstd, r_x, r_st)
                        tr.op("dve", lambda e, rows=rows: e.scalar_tensor_tensor(
                            out=xs_bf[:rows], in0=x_tm[:rows], scalar=rstd[:rows, 0:1], in1=g_bc[:rows],
                            op0=ALU.mult, op1=ALU.mult), [r_x, r_st, r_const], [r_xs])
                        p, rp = nxt_pT()
                        transposes(xs_bf, rows, KC, p, rp,
                                   lambda j0, n, ti=ti, rows=rows: xnT[:, j0:j0 + n, ti * 128:ti * 128 + rows],
                                   r_xs, r_xnT)

                def k_post(ti, rows, pos0, c0, outs, orow):
                    tr.dma("sp", tabA[:rows, 0:2 * HA * 8], I["cosA"][pos0:pos0 + rows, :], writes=[r_tabA], lane_res=r_tabA)
                    tr.dma("sp", tabA[:rows, 2 * HA * 8:], I["sinA"][pos0:pos0 + rows, :], writes=[r_tabA], lane_res=r_tabA)
                    rope(kf[ti], rows, 2 * HA, 64, 8, tabA[:, 0:2 * HA * 8], tabA[:, 2 * HA * 8:], rtmp, r_kf[ti], r_tabA, r_rtmp)
                    if outs is not None:
                        tr.dma("sp", outs["dk"][orow:orow + rows, :], kf[ti][:rows, :NKA], reads=[r_kf[ti]], lane_res=r_kf[ti])
                    act_copy(kbf[:rows, :NKA], kf[ti][:rows, :NKA], [r_kf[ti]], [r_kbf])
                    p, rp = nxt_pT()
                    transposes(kbf, rows, NKA // 128, p, rp,
                               lambda j0, n: stage[:, j0:j0 + n, c0:c0 + rows], r_kbf, r_stage)

                def v_post(ti, rows, outs, orow):
                    if outs is not None:
                        tr.dma("sp", outs["dv"][orow:orow + rows, :], vf[ti][:rows, :NVA], reads=[r_vf[ti]], lane_res=r_vf[ti])
                    act_copy(vbf[:rows, :NVA], vf[ti][:rows, :NVA], [r_vf[ti]], [r_vbf], "dve")

                def c_post(ti, rows, pos0, outs, orow):
                    rstd_of(cf[ti][:rows, :KVL], rows, KVL, cbf, r_cbf, ss2, rstd2, r_cf[ti], r_st2)
                    tr.op("dve", lambda e: e.scalar_tensor_tensor(
                        out=cf[ti][:rows, :KVL], in0=cf[ti][:rows, :KVL], scalar=rstd2[:rows, 0:1], in1=g_kva[:rows],
                        op0=ALU.mult, op1=ALU.mult), [r_st2, r_const], [r_cf[ti]])
                    tr.dma("sp", tabB[:rows, 0:32], I["cosB"][pos0:pos0 + rows, 0:32], writes=[r_tabB], lane_res=r_tabB)
                    tr.dma("sp", tabB[:rows, 32:64], I["sinB"][pos0:pos0 + rows, 0:32], writes=[r_tabB], lane_res=r_tabB)
                    rope(cf[ti][:, KVL:KVL + 64], rows, 1, 64, 32, tabB[:, 0:32], tabB[:, 32:64], rtmp, r_cf[ti], r_tabB, r_rtmp)
                    if outs is not None:
                        tr.dma("sp", outs["ckv"][orow:orow + rows, :], cf[ti][:rows, :KVL], reads=[r_cf[ti]], lane_res=r_cf[ti])
                        tr.dma("sp", outs["kpe"][orow:orow + rows, :], cf[ti][:rows, KVL:KVL + 64], reads=[r_cf[ti]], lane_res=r_cf[ti])
                    act_copy(cbf[:rows], cf[ti][:rows, :KVL + 64], [r_cf[ti]], [r_cbf])

                def latent_post(rows, c0):
                    p, rp = nxt_pT()
                    transposes(cbf, rows, KVL // 128, p, rp,
                               lambda j0, n: ckvT[:, j0:j0 + n, c0:c0 + rows], r_cbf, r_ckvT)
                    p, rp = nxt_pT()
                    transposes(cbf[:, KVL:], rows, 1, p, rp,
                               lambda j0, n: kpeT[:, 0:1, c0:c0 + rows], r_cbf, r_kpeT, chw=64)

                def kvb_group(tiles, ntok, sc, col0):
                    def ev_n(m, mw, acc, r_acc):
                        any_copy(stage2[:mw, m, :ntok], acc[:mw, :ntok], [r_acc], [r_stage2])
                    gemm_fm(ckvT, r_ckvT, KVL // 128, ntok, I["w_kvn"], 0, NKB, pacc, r_pacc, ev_n)
                    tr.dma("sp", sc["KBT"][:, col0:col0 + ntok].rearrange("(m p) t -> p m t", p=128),
                           stage2[:, :NKB // 128, :ntok], reads=[r_stage2], lane_res=r_stage2)

                    def ev_v(ti, b0, w, acc, r_acc):
                        t0, rows = tiles[ti]
                        any_copy(vbb[:rows, b0:b0 + w], acc[:rows, :w], [r_acc], [r_vbb])
                        if b0 + w >= NVB:
                            tr.dma("sp", sc["VB"][col0 + t0:col0 + t0 + rows, :], vbb[:rows, :NVB], reads=[r_vbb], lane_res=r_vbb)
                    for ti, (t0, rows) in enumerate(tiles):
                        gemm_tm(ckvT, r_ckvT, KVL // 128, [(t0, rows)], I["w_kvv"], 0, NVB, [pacc[ti]], [r_pacc[ti]],
                                lambda _ti, b0, w, acc, r_acc, ti=ti: ev_v(ti, b0, w, acc, r_acc))

                def k_group(tiles_src, pos0, sc, col0, outs, out_rows):
                    tiles = [(i * 128, rows) for i, (_, rows) in enumerate(tiles_src)]
                    ntok = tiles[-1][0] + tiles[-1][1]
                    make_xnT(tiles_src, g_attn)

                    def ev_k(ti, b0, w, acc, r_acc):
                        act_copy(kf[ti][:tiles[ti][1], b0:b0 + w], acc[:tiles[ti][1], :w], [r_acc], [r_kf[ti]])

                    def ev_v(ti, b0, w, acc, r_acc):
                        act_copy(vf[ti][:tiles[ti][1], b0:b0 + w], acc[:tiles[ti][1], :w], [r_acc], [r_vf[ti]], "dve")

                    def ev_c(ti, b0, w, acc, r_acc):
                        act_copy(cf[ti][:tiles[ti][1], b0:b0 + w], acc[:tiles[ti][1], :w], [r_acc], [r_cf[ti]])
                    def oo(ti):
                        o = outs if (outs is not None and out_rows[ti] is not None) else None
                        return o, (out_rows[ti] if o is not None else 0)
                    gemm_tm(xnT, r_xnT, KC, tiles, I["w_in"], o_k, NKA, pacc, r_pacc, ev_k)
                    for ti, (t0, rows) in enumerate(tiles):
                        k_post(ti, rows, pos0 + t0, t0, *oo(ti))
                    tr.dma("sp", sc["KT"][:, col0:col0 + ntok].rearrange("(m p) t -> p m t", p=128),
                           stage[:, :NKA // 128, :ntok], reads=[r_stage], lane_res=r_stage)
                    gemm_tm(xnT, r_xnT, KC, tiles, I["w_in"], o_v, NVA, pacc, r_pacc, ev_v)
                    for ti, (t0, rows) in enumerate(tiles):
                        v_post(ti, rows, *oo(ti))
                        tr.dma("sp", sc["V"][col0 + t0:col0 + t0 + rows, :], vbf[:rows, :NVA], reads=[r_vbf], lane_res=r_vbf)
                    gemm_tm(xnT, r_xnT, KC, tiles, I["w_in"], o_ckv, KVL + 64, pacc, r_pacc, ev_c)
                    for ti, (t0, rows) in enumerate(tiles):
                        c_post(ti, rows, pos0 + t0, *oo(ti))
                        latent_post(rows, t0)
                    tr.dma("sp", sc["KPET"][:, col0:col0 + ntok], kpeT[:, 0, :ntok], reads=[r_kpeT], lane_res=r_kpeT)
                    kvb_group(tiles, ntok, sc, col0)

                scP = dict(KT=S["KT_p"], V=S["V_p"], KBT=S["KBT_p"], KPET=S["KPET_p"], VB=S["VB_p"])
                scS = dict(KT=S["KT_s"], V=S["V_s"], KBT=S["KBT_s"], KPET=S["KPET_s"], VB=S["VB_s"])
                outP = dict(dk=O["dk_p"], dv=O["dv_p"], ckv=O["ckv_p"], kpe=O["kpe_p"])
                outS = dict(dk=O["dk_s"], dv=O["dv_s"], ckv=O["ckv_s"], kpe=O["kpe_s"])
                for g in range(NKT // 4):
                    tl = [(I["xk"][(4 * g + i) * 128:(4 * g + i + 1) * 128, :], 128) for i in range(4)]
                    orows = [((4 * g + i - 1) * 128 if 1 <= 4 * g + i < NQT else None) for i in range(4)]
                    k_group(tl, g * 512, scP, g * 512, outP, orows)
                k_group([(I["xs"][:, :], 64)], SEQ, scS, PAST, outS, [0])
                for g in range(PAST // 512 if PAST >= 512 else 1):
                    gt = min(4, PAST // 128)
                    for i in range(gt):
                        r0 = (g * 4 + i) * 128
                        tr.dma("pool", kbf[:, :NKA], I["c_dk"][r0:r0 + 128, :], writes=[r_kbf], lane_res=r_kbf)
                        p, rp = nxt_pT()
                        transposes(kbf, 128, NKA // 128, p, rp,
                                   lambda j0, n, i=i: stage[:, j0:j0 + n, i * 128:(i + 1) * 128], r_kbf, r_stage)
                        tr.dma("pool", vbf[:, :NVA], I["c_dv"][r0:r0 + 128, :], writes=[r_vbf], lane_res=r_vbf)
                        tr.dma("sp", S["V_s"][r0:r0 + 128, :], vbf[:, :NVA], reads=[r_vbf], lane_res=r_vbf)
                        tr.dma("pool", cbf[:, :KVL], I["c_ckv"][r0:r0 + 128, :], writes=[r_cbf], lane_res=r_cbf)
                        tr.dma("pool", cbf[:, KVL:], I["c_kpe"][r0:r0 + 128, :], writes=[r_cbf], lane_res=r_cbf)
                        latent_post(128, i * 128)
                    nt = gt * 128
                    tr.dma("sp", S["KT_s"][:, g * 512:g * 512 + nt].rearrange("(m p) t -> p m t", p=128),
                           stage[:, :NKA // 128, :nt], reads=[r_stage], lane_res=r_stage)
                    tr.dma("sp", S["KPET_s"][:, g * 512:g * 512 + nt], kpeT[:, 0, :nt], reads=[r_kpeT], lane_res=r_kpeT)
                    kvb_group([(i * 128, 128) for i in range(gt)], nt, scS, g * 512)
                tr.barrier()


        phase1()
        PHASES = cfg.get("phases", 4)
        QTILES = [(I["xk"][j * 128:(j + 1) * 128, :], 128, j * 128, j * 128) for j in range(NQT)]
        QTILES.append((I["xs"][:, :], 64, SEQ, NPQ))
        QGROUPS = [QTILES[i:i + 4] for i in range(0, len(QTILES), 4)]
        def make_xnT2(tiles_src, g_bc, x_tm, r_x, xs_bf, r_xs, ss, rstd, r_st, xnT, r_xnT):
            for ti, (src, rows) in enumerate(tiles_src):
                tr.dma("sp", x_tm[:rows], src, writes=[r_x], lane_res=r_x)
                rstd_of(x_tm[:rows, :], rows, D, xs_bf, r_xs, ss, rstd, r_x, r_st)
                tr.op("dve", lambda e, rows=rows: e.scalar_tensor_tensor(
                    out=xs_bf[:rows], in0=x_tm[:rows], scalar=rstd[:rows, 0:1], in1=g_bc[:rows],
                    op0=ALU.mult, op1=ALU.mult), [r_x, r_st, r_const], [r_xs])
                p, rp = nxt_pT()
                transposes(xs_bf, rows, KC, p, rp,
                           lambda j0, n, ti=ti, rows=rows: xnT[:, j0:j0 + n, ti * 128:ti * 128 + rows], r_xs, r_xnT)

        def phase2():
            if PHASES >= 2:
              with ExitStack() as st2_:
                st2 = stack if cfg.get('noalias') else st2_
                set_wslots(st2, 256)
                g_attn = sb("g_attn2", [128, D], F32, st2)
                g_qa = sb("g_qa", [128, QL], F32, st2)
                b_gate = sb("b_gate", [128, 2 * KC], F32, st2)
                for t_, n_ in ((g_attn, "g_attn"), (g_qa, "g_qa"), (b_gate, "b_gate")):
                    if cfg.get("p2cut", 9) < 0:
                        continue
                    tr.dma("sp", t_[:], I[n_][:], writes=[r_const], lane_res=r_const)
                x_tm = sb("x_tm2", [128, D], F32, st2); r_x = Res("x_tm2")
                xs_bf = sb("xs_bf2", [128, D], BF16, st2); r_xs = Res("xs_bf2")
                xnT = sb("xnT2", [128, KC, 512], BF16, st2); r_xnT = Res("xnT2")
                ss = sb("ss_2", [128, 1], F32, st2); rstd = sb("rstd_2", [128, 1], F32, st2); r_st = Res("stat_2")
                qf = sb("qf", [128, max(NKA, NQB)], F32, st2); r_qf = Res("qf")
                qbf = sb("qbf", [128, max(NKA, NQB)], BF16, st2); r_qbf = Res("qbf")
                cqT = sb("cqT", [128, QL // 128, 128], BF16, st2); r_cqT = Res("cqT")
                tabA = sb("tabA2", [128, 2 * 2 * HA * 8], F32, st2); r_tabA = Res("tabA2")
                tabB = sb("tabB2", [128, 2 * HB * 32], F32, st2); r_tabB = Res("tabB2")
                rtmp = sb("rtmp2", [128, 4 * max(2 * HA * 8, HB * 32)], F32, st2); r_rtmp = Res("rtmp2")
                stQ = sb("stQ", [128, NKA // 128, 512], BF16, st2); r_stQ = Res("stQ")
                stN = sb("stN", [128, HB, 512], BF16, st2); r_stN = Res("stN")
                stP = sb("stP", [64, HB, 512], BF16, st2); r_stP = Res("stP")
                gbuf = [sb(f"gbuf{i}", [128, 512], BF16, st2) for i in range(2)]; r_gbuf = [Res(f"gbuf{i}") for i in range(2)]
                def do_group2(grp):
                    tiles = [(i * 128, rows) for i, (_, rows, _, _) in enumerate(grp)]
                    ntok = tiles[-1][0] + tiles[-1][1]
                    col0 = grp[0][3]
                    if cfg.get("p2cut", 9) < 1:
                        return
                    make_xnT2([(g_[0], g_[1]) for g_ in grp], g_attn, x_tm, r_x, xs_bf, r_xs, ss, rstd, r_st, xnT, r_xnT)

                    def ev_g(m, mw, acc, r_acc, ntok=ntok, col0=col0):
                        i = m % 2
                        tr.op("act", lambda e: e.activation(out=gbuf[i][:mw, :ntok], in_=acc[:mw, :ntok], func=AF.Sigmoid,
                                                            bias=b_gate[:mw, m:m + 1]), [r_acc, r_const], [r_gbuf[i]])
                        tr.dma("sp", S["GT"][m * 128:m * 128 + mw, col0:col0 + ntok], gbuf[i][:mw, :ntok],
                               reads=[r_gbuf[i]], lane_res=r_gbuf[i])
                    if cfg.get("p2cut", 9) >= 2:
                        gemm_fm(xnT, r_xnT, KC, ntok, I["w_in"], o_g, 2 * D, pacc, r_pacc, ev_g)
                    for ti, (t0, rows) in enumerate(tiles):
                        if cfg.get("p2cut", 9) < 3:
                            break
                        pos0 = grp[ti][2]
                        one = [(t0, rows)]
                        gemm_tm(xnT, r_xnT, KC, one, I["w_in"], o_q, NKA, [pacc[0]], [r_pacc[0]],
                                lambda _t, b0, w, acc, r_acc, rows=rows: act_copy(qf[:rows, b0:b0 + w], acc[:rows, :w], [r_acc], [r_qf]))
                        tr.dma("sp", tabA[:rows, 0:2 * HA * 8], I["cosA"][pos0:pos0 + rows, :], writes=[r_tabA], lane_res=r_tabA)
                        tr.dma("sp", tabA[:rows, 2 * HA * 8:], I["sinA"][pos0:pos0 + rows, :], writes=[r_tabA], lane_res=r_tabA)
                        rope(qf, rows, 2 * HA, 64, 8, tabA[:, 0:2 * HA * 8], tabA[:, 2 * HA * 8:], rtmp, r_qf, r_tabA, r_rtmp)
                        act_copy(qbf[:rows, :NKA], qf[:rows, :NKA], [r_qf], [r_qbf])
                        p, rp = nxt_pT()
                        transposes(qbf, rows, NKA // 128, p, rp,
                                   lambda j0, n, t0=t0, rows=rows: stQ[:, j0:j0 + n, t0:t0 + rows], r_qbf, r_stQ)
                        gemm_tm(xnT, r_xnT, KC, one, I["w_in"], o_cq, QL, [pacc[1]], [r_pacc[1]],
                                lambda _t, b0, w, acc, r_acc, rows=rows: act_copy(qf[:rows, b0:b0 + w], acc[:rows, :w], [r_acc], [r_qf]))
                        rstd_of(qf[:rows, :QL], rows, QL, qbf, r_qbf, ss, rstd, r_qf, r_st)
                        tr.op("dve", lambda e, rows=rows: e.scalar_tensor_tensor(
                            out=qbf[:rows, :QL], in0=qf[:rows, :QL], scalar=rstd[:rows, 0:1], in1=g_qa[:rows],
                            op0=ALU.mult, op1=ALU.mult), [r_qf, r_st, r_const], [r_qbf])
                        p, rp = nxt_pT()
                        transposes(qbf, rows, QL // 128, p, rp,
                                   lambda j0, n, rows=rows: cqT[:, j0:j0 + n, :rows], r_qbf, r_cqT)
                        gemm_tm(cqT, r_cqT, QL // 128, [(0, rows)], I["w_qb"], 0, NQB, [pacc[2]], [r_pacc[2]],
                                lambda _t, b0, w, acc, r_acc, rows=rows: act_copy(qf[:rows, b0:b0 + w], acc[:rows, :w], [r_acc], [r_qf]))
                        tr.dma("sp", tabB[:rows, 0:HB * 32], I["cosB"][pos0:pos0 + rows, :], writes=[r_tabB], lane_res=r_tabB)
                        tr.dma("sp", tabB[:rows, HB * 32:], I["sinB"][pos0:pos0 + rows, :], writes=[r_tabB], lane_res=r_tabB)
                        rope(qf, rows, HB, 192, 32, tabB[:, 0:HB * 32], tabB[:, HB * 32:], rtmp, r_qf, r_tabB, r_rtmp, off=128)
                        act_copy(qbf[:rows, :NQB], qf[:rows, :NQB], [r_qf], [r_qbf])
                        p, rp = nxt_pT()
                        transposes(qbf, rows, HB, p, rp,
                                   lambda j0, n, t0=t0, rows=rows: stN[:, j0:j0 + n, t0:t0 + rows], r_qbf, r_stN, stride=192)
                        p, rp = nxt_pT()
                        transposes(qbf, rows, HB, p, rp,
                                   lambda j0, n, t0=t0, rows=rows: stP[:, j0:j0 + n, t0:t0 + rows], r_qbf, r_stP,
                                   chw=64, stride=192, off=128)
                    if cfg.get("p2cut", 9) < 3:
                        return
                    tr.dma("sp", S["QT"][:, col0:col0 + ntok].rearrange("(m p) t -> p m t", p=128), stQ[:, :, :ntok],
                           reads=[r_stQ], lane_res=r_stQ)
                    tr.dma("sp", S["QBN"][:, col0:col0 + ntok].rearrange("(m p) t -> p m t", p=128), stN[:, :, :ntok],
                           reads=[r_stN], lane_res=r_stN)
                    tr.dma("sp", S["QBP"][:, col0:col0 + ntok].rearrange("(m p) t -> p m t", p=64), stP[:, :, :ntok],
                           reads=[r_stP], lane_res=r_stP)
                for grp in QGROUPS:
                    do_group2(grp)
                tr.barrier()

        phase2()
        def phase3():
            if PHASES >= 3:
              with ExitStack() as st3_:
                st3 = stack if cfg.get('noalias') else st3_
                NKP = SEQ
                NKS_ = NKTS * 128
                visb = sb("visb", [128, NKT], F32, st3)
                lamt = sb("lamt", [128, 256], F32, st3)
                g_sub = sb("g_sub", [128, 128], F32, st3)
                zero_c = sb("zero_c", [128, 1], F32, st3)
                for t_, n_ in ((visb, "visb"), (lamt, "lam"), (g_sub, "g_sub")):
                    tr.dma("sp", t_[:], I[n_][:], writes=[r_const], lane_res=r_const)
                lw = sb("lw", [128, 64], F32, st3); lam_e = sb("lam_e", [128, 4], F32, st3); r_lam = Res("lam")
                tr.op("dve", lambda e: e.memset(zero_c[:], 0.0), [], [r_const])
                tr.op("dve", lambda e: e.memset(lam_e[:], 0.0), [], [r_lam])
                for m in range(2):
                    tr.op("dve", lambda e, m=m: e.tensor_tensor(out=lw[:], in0=lamt[:, 128 * m:128 * m + 64],
                                                                in1=lamt[:, 128 * m + 64:128 * m + 128], op=ALU.mult),
                          [r_const], [r_lam])
                    tr.op("dve", lambda e, m=m: e.tensor_reduce(out=lam_e[:, m:m + 1], in_=lw[:], axis=AX.X, op=ALU.add),
                          [r_lam], [r_lam])
                tr.op("act", lambda e: e.activation(out=lam_e[:, 0:2], in_=lam_e[:, 0:2], func=AF.Exp), [r_lam], [r_lam])
                tr.op("dve", lambda e: e.tensor_tensor(out=lam_e[:, 2:3], in0=lam_e[:, 1:2], in1=lam_e[:, 0:1], op=ALU.subtract),
                      [r_lam], [r_lam])
                tr.op("dve", lambda e: e.tensor_scalar(out=lam_e[:, 3:4], in0=lam_e[:, 2:3], scalar1=-LAM_INIT, scalar2=None,
                                                       op0=ALU.add), [r_lam], [r_lam])
                tr.op("dve", lambda e: e.tensor_scalar(out=g_sub[:], in0=g_sub[:], scalar1=1.0 - LAM_INIT, scalar2=None,
                                                       op0=ALU.mult), [r_const], [r_const])
                kA = sb("kA", [128, NKP], BF16, st3); r_kA = Res("kA")
                vA = sb("vA", [128, NKT, 136], BF16, st3); r_vA = Res("vA")
                qA = sb("qA", [128, 2, NQ], BF16, st3); r_qA = Res("qA")
                kB = sb("kB", [128, NKP], BF16, st3); r_kB = Res("kB")
                kP = sb("kP", [128, NKP], BF16, st3); r_kP = Res("kP")
                kPs = sb("kPs", [128, NKS_], BF16, st3); r_kPs = Res("kPs")
                qBn = sb("qBn", [128, NQ], BF16, st3); r_qBn = Res("qBn")
                qBp = sb("qBp", [128, NQ], BF16, st3); r_qBp = Res("qBp")
                NPB = 5
                et = [sb(f"et{i}", [128, 512], BF16, st3) for i in range(NPB)]; r_et = [Res(f"et{i}") for i in range(NPB)]
                o1 = sb("o1", [128, 128], F32, st3); o2 = sb("o2", [128, 128], F32, st3); r_o = Res("o12")
                rr = sb("rr", [128, 4], F32, st3)
                ob = [sb(f"ob{i}", [128, 128], BF16, st3) for i in range(2)]; r_ob = [Res(f"ob{i}") for i in range(2)]
                ssa = sb("ssa", [128, 1], F32, st3); rsa = sb("rsa", [128, 1], F32, st3); r_sa = Res("sa")
                jnk = sb("jnk3", [128, 128], BF16, st3); r_jnk = Res("jnk3")
                pst = [pacc[0], pacc[1], ps("pst2", [128, 512], F32, st3)]; r_pst = [r_pacc[0], r_pacc[1], Res("pst2")]
                for i_ in range(2):
                    pst.append(pT[i_][:].bitcast(F32).rearrange("p a b -> p (a b)"))
                    r_pst.append(r_pT[i_])
                pab = [pacc[2], pacc[3], ps("pab2", [128, 512], F32, st3)]
                r_pab = [Res(f"pab{i}") for i in range(3)]
                ACC = [(pab[i // 3][:, (i % 3) * 129:(i % 3 + 1) * 129], r_pab[i // 3]) for i in range(8)]
                tr.op("dve", lambda e: e.memset(vA[:, :, 128:129], 1.0), [], [r_vA])
                tr.op("dve", lambda e: e.memset(qA[:], 0.0), [], [r_qA])
                tr.op("dve", lambda e: e.memset(kP[64:128, :], 0.0), [], [r_kP])
                tr.op("dve", lambda e: e.memset(kPs[64:128, :], 0.0), [], [r_kPs])
                tr.op("dve", lambda e: e.memset(qBp[64:128, :], 0.0), [], [r_qBp])
                tr.dma("sp", kP[0:64, :], S["KPET_p"][:, :], writes=[r_kP], lane_res=r_kP)
                tr.dma("sp", kPs[0:64, :], S["KPET_s"][:, :], writes=[r_kPs], lane_res=r_kPs)
                st_rr = [0]
                ob_rr = [0]

                def attend(parts, r_parts, nkt, k_last_rows, q_tiles, accs, scale, prompt):
                    c_lo = q_tiles[0][0]
                    c_hi = q_tiles[-1][0] + q_tiles[-1][1]
                    for (acc, r_acc) in accs:
                        tr.op("dve", lambda e, acc=acc: e.memset(acc, 0.0), [], [r_acc])
                    pending = []
                    for i in range(nkt):
                        krows = k_last_rows if i == nkt - 1 else 128
                        if prompt and i < NQT:
                            qt = [(k_, t) for k_, t in enumerate(q_tiles) if t[2] >= i]
                        else:
                            qt = list(enumerate(q_tiles))
                        if not qt:
                            continue
                        a = qt[0][1][0]
                        n = c_hi - a
                        si = st_rr[0] % NPB
                        st_rr[0] += 1

                        def f(e, i=i, krows=krows, a=a, n=n, si=si):
                            ins = None
                            for pi, (k_ap, q_ap) in enumerate(parts):
                                ins = e.matmul(pst[si][:krows, :n], lhsT=k_ap[:, i * 128:i * 128 + krows], rhs=q_ap[:, a:a + n],
                                               start=(pi == 0), stop=(pi == len(parts) - 1))
                            return ins
                        tr.op("pe", f, r_parts, [r_pst[si]])
                        bias = visb[:krows, i:i + 1] if prompt else zero_c[:krows, 0:1]
                        tr.op("act", lambda e, krows=krows, n=n, si=si, bias=bias: e.activation(
                            out=et[si][:krows, :n], in_=pst[si][:krows, :n], func=AF.Exp, bias=bias, scale=scale),
                            [r_pst[si], r_const], [r_et[si]])
                        if prompt and i < NQT:
                            for k_, t in qt:
                                if t[2] == i:
                                    tr.op("dve", lambda e, c=t[0] - a, si=si: e.memset(et[si][64:128, c:c + 64], 0.0),
                                          [], [r_et[si]])

                        def pv(qt=qt, a=a, si=si, i=i, krows=krows):
                            def g(e):
                                ins = None
                                for k_, t in qt:
                                    ins = e.matmul(accs[k_][0][:t[1], :], lhsT=et[si][:krows, t[0] - a:t[0] - a + t[1]],
                                                   rhs=r_vcur[0][:krows, i, 0:129], start=False, stop=False, skip_group_check=True)
                                return ins
                            wr = list({id(accs[k_][1]): accs[k_][1] for k_, t in qt}.values())
                            tr.op("pe", g, [r_et[si], r_vcur[1]], wr)
                        pending.append(pv)
                        if len(pending) > NPB - 1:
                            pending.pop(0)()
                    for pv_ in pending:
                        pv_()

                r_vcur = [None, None]
                PCH = [[(j * 128, 128, j) for j in range(c0, min(c0 + 4, NQT))] for c0 in range(0, NQT, 4)]
                SCH = [[(NPQ, 64, 0)]]

                def store_o(dst, h, t, src_fn):
                    i = ob_rr[0] % 2
                    ob_rr[0] += 1
                    src_fn(ob[i], r_ob[i])
                    tr.dma("sp", dst[t[0]:t[0] + t[1], h * 128:(h + 1) * 128], ob[i][:t[1], :], reads=[r_ob[i]], lane_res=r_ob[i])

                for h in range(HA):
                    tr.dma("sp", qA[0:64, 0, :], S["QT"][h * 64:(h + 1) * 64, :], writes=[r_qA], lane_res=r_qA)
                    tr.dma("sp", qA[64:128, 1, :], S["QT"][(HA + h) * 64:(HA + h + 1) * 64, :], writes=[r_qA], lane_res=r_qA)
                    for (prompt, sc, nk, nkt, klast, chunks) in ((True, "p", NKP, NKT, 128, PCH),
                                                              (False, "s", NKS_, NKTS, TKS - (NKTS - 1) * 128, SCH)):
                        tr.dma("sp", kA[0:64, :nk], S["KT_" + sc][h * 64:(h + 1) * 64, :], writes=[r_kA], lane_res=r_kA)
                        tr.dma("sp", kA[64:128, :nk], S["KT_" + sc][(HA + h) * 64:(HA + h + 1) * 64, :], writes=[r_kA], lane_res=r_kA)
                        tr.dma("sp", vA[:, :nkt, 0:128], S["V_" + sc][:, h * 128:(h + 1) * 128].rearrange("(i p) d -> p i d", p=128),
                               writes=[r_vA], lane_res=r_vA)
                        r_vcur[0], r_vcur[1] = vA, r_vA
                        for ch in chunks:
                            a1 = ACC[0:len(ch)]
                            a2 = ACC[4:4 + len(ch)]
                            attend([(kA[:, :], qA[:, 0, :])], [r_kA, r_qA], nkt, klast, ch, a1, SC_A, prompt)
                            attend([(kA[:, :], qA[:, 1, :])], [r_kA, r_qA], nkt, klast, ch, a2, SC_A, prompt)
                            for k_, t in enumerate(ch):
                                rows = t[1]
                                (c1, r1), (c2, r2) = a1[k_], a2[k_]
                                tr.op("dve", lambda e, c1=c1, rows=rows: e.reciprocal(out=rr[:rows, 0:1], in_=c1[:rows, 128:129]), [r1], [r_o])
                                tr.op("dve", lambda e, c2=c2, rows=rows: e.reciprocal(out=rr[:rows, 1:2], in_=c2[:rows, 128:129]), [r2], [r_o])
                                tr.op("dve", lambda e, rows=rows: e.tensor_tensor(out=rr[:rows, 2:3], in0=rr[:rows, 1:2], in1=lam_e[:rows, 3:4],
                                                                                  op=ALU.mult), [r_lam], [r_o])
                                tr.op("dve", lambda e, c1=c1, rows=rows: e.tensor_scalar(out=o1[:rows, :], in0=c1[:rows, 0:128], scalar1=rr[:rows, 0:1],
                                                                                       scalar2=None, op0=ALU.mult), [r1], [r_o])
                                tr.op("dve", lambda e, c2=c2, rows=rows: e.scalar_tensor_tensor(out=o2[:rows, :], in0=c2[:rows, 0:128], scalar=rr[:rows, 2:3],
                                                                                              in1=o1[:rows, :], op0=ALU.mult, op1=ALU.add), [r2], [r_o])
                                rstd_of(o2[:rows, :], rows, 128, jnk, r_jnk, ssa, rsa, r_o, r_sa)
                                store_o(S["OA"], h, t, lambda o_, r_o_, rows=rows: tr.op("dve", lambda e: e.scalar_tensor_tensor(
                                    out=o_[:rows, :], in0=o2[:rows, :], scalar=rsa[:rows, 0:1], in1=g_sub[:rows, :],
                                    op0=ALU.mult, op1=ALU.mult), [r_o, r_sa, r_const], [r_o_]))
                for h in range(HB):
                    tr.dma("sp", qBn[:, :], S["QBN"][h * 128:(h + 1) * 128, :], writes=[r_qBn], lane_res=r_qBn)
                    tr.dma("sp", qBp[0:64, :], S["QBP"][h * 64:(h + 1) * 64, :], writes=[r_qBp], lane_res=r_qBp)
                    for (prompt, sc, nk, nkt, klast, chunks, kp_, r_kp_) in ((True, "p", NKP, NKT, 128, PCH, kP, r_kP),
                                                                          (False, "s", NKS_, NKTS, TKS - (NKTS - 1) * 128, SCH, kPs, r_kPs)):
                        tr.dma("sp", kB[:, :nk], S["KBT_" + sc][h * 128:(h + 1) * 128, :], writes=[r_kB], lane_res=r_kB)
                        tr.dma("sp", vA[:, :nkt, 0:128], S["VB_" + sc][:, h * 128:(h + 1) * 128].rearrange("(i p) d -> p i d", p=128),
                               writes=[r_vA], lane_res=r_vA)
                        r_vcur[0], r_vcur[1] = vA, r_vA
                        for ch in chunks:
                            a1 = ACC[0:len(ch)]
                            attend([(kB[:, :], qBn[:, :]), (kp_[:, :], qBp[:, :])], [r_kB, r_qBn, r_kp_, r_qBp], nkt, klast, ch, a1, SC_B, prompt)
                            for k_, t in enumerate(ch):
                                rows = t[1]
                                c1, r1 = a1[k_]
                                tr.op("dve", lambda e, c1=c1, rows=rows: e.reciprocal(out=rr[:rows, 0:1], in_=c1[:rows, 128:129]), [r1], [r_o])
                                store_o(S["OB"], h, t, lambda o_, r_o_, rows=rows, c1=c1, r1=r1: tr.op("dve", lambda e: e.tensor_scalar(
                                    out=o_[:rows, :], in0=c1[:rows, 0:128], scalar1=rr[:rows, 0:1], scalar2=None, op0=ALU.mult),
                                    [r1, r_o], [r_o_]))
                tr.barrier()

        phase3()
        def phase4():
            if PHASES >= 4:
              Hs = dscr("Hs", [NQ, D], F32)
              Ys = dscr("Ys", [NQ, D], F32)
              with ExitStack() as st4:
                set_wslots(st4, 256)
                halves = [(wslots[i // 2][:, :, (i % 2) * 128:(i % 2) * 128 + 128], r_ws[i // 2][i % 2]) for i in range(2 * NWS)]
                hh = [0]

                def load_half(W, f):
                    i = hh[0] % len(halves)
                    hh[0] += 1
                    ap, r = halves[i]
                    src = W[:, f * 128:(f + 1) * 128].rearrange("(k p) c -> p k c", p=128)
                    cached_load((id(W), "half", f), ap[:, :KC, :], src, [128, KC, 128], [r], r)
                    return ap, r
                gbc = sb("gbc", [128, D], F32, st4); r_gbc = Res("gbc")
                wcv = sb("wcv", [128, 3 * NF], F32, st4); bcv = sb("bcv", [128, NF], F32, st4)
                stc = sb("stc", [128, 2 * NF], F32, st4); flag = sb("flag", [128, 1], F32, st4)
                identf = sb("identf", [128, 128], F32, st4)
                for t_, n_ in ((wcv, "w_conv"), (bcv, "b_conv"), (stc, "st_conv"), (flag, "flag"), (identf, "identf")):
                    tr.dma("sp", t_[:], I[n_][:], writes=[r_const], lane_res=r_const)
                tr.dma("sp", gbc[:], I["g_ffn"][:], writes=[r_gbc], lane_res=r_gbc)
                htile = sb("htile", [128, D], F32, st4); r_h = Res("htile")
                xs_bf = sb("xs_bf4", [128, max(D, NVA, NVB)], BF16, st4); r_xs = Res("xs_bf4")
                ss = sb("ss_4", [128, 1], F32, st4); rstd = sb("rstd_4", [128, 1], F32, st4); r_st = Res("stat_4")
                big = sb("big", [128, max(NF, KC) * 512], BF16, st4); r_big = Res("big")
                mT = big[:, :KC * 512].rearrange("p (k t) -> p k t", t=512)
                actT = big[:, :NF * 512].rearrange("p (k t) -> p k t", t=512)
                hnT = sb("hnT", [128, max(KC, NVA // 128, NVB // 128), 512], BF16, st4); r_hnT = Res("hnT")
                gb = sb("gb", [128, 512], BF16, st4); r_gb = Res("gb")
                ugP = sb("ugP", [128, 2 + 512], F32, st4); ugS = sb("ugS", [128, 2 + 64], F32, st4); r_ug = Res("ug")
                cb = sb("cb", [128, 512], F32, st4); r_cb = Res("cb")
                carry = sb("carry", [128, NF, 2], F32, st4); r_carry = Res("carry")
                cv = sb("cv", [128, 4, NF], F32, st4); r_cv = Res("cv")
                cvo = sb("cvo", [128, 128], F32, st4); r_cvo = Res("cvo")
                hblk = sb("hblk", [128, WB], F32, st4); r_hblk = Res("hblk")
                yblk = sb("yblk", [128, WB], F32, st4); r_yblk = Res("yblk")
                r_Hs = Res("Hs")
                tr.op("dve", lambda e: e.memset(carry[:], 0.0), [], [r_carry])
                def do_group4(gi, grp):
                    tiles = [(i * 128, rows) for i, (_, rows, _, _) in enumerate(grp)]
                    ntok = tiles[-1][0] + tiles[-1][1]
                    col0 = grp[0][3]
                    has_s = grp[-1][1] == 64
                    npr = ntok - 64 if has_s else ntok
                    last = gi == len(QGROUPS) - 1
                    for (Osc, Wb, nfe, first) in ((S["OA"], I["w_ba"], NVA, True), (S["OB"], I["w_bb"], NVB, False)):
                        for ti, (t0, rows) in enumerate(tiles):
                            tr.dma("sp", xs_bf[:rows, :nfe], Osc[col0 + t0:col0 + t0 + rows, :], writes=[r_xs], lane_res=r_xs)
                            p, rp = nxt_pT()
                            transposes(xs_bf, rows, nfe // 128, p, rp,
                                       lambda j0, n, t0=t0, rows=rows: hnT[:, j0:j0 + n, t0:t0 + rows], r_xs, r_hnT)

                        def ev_y(m, mw, acc, r_acc, first=first, ntok=ntok, col0=col0):
                            g_row = (0 if first else D) + m * 128
                            tr.dma("sp", gb[:mw, :ntok], S["GT"][g_row:g_row + mw, col0:col0 + ntok], writes=[r_gb], lane_res=r_gb)
                            if first:
                                tr.op("dve", lambda e: e.tensor_tensor(out=mT[:mw, m, :ntok], in0=acc[:mw, :ntok], in1=gb[:mw, :ntok],
                                                                       op=ALU.mult), [r_acc, r_gb], [r_big])
                            else:
                                tr.op("dve", lambda e: e.tensor_tensor(out=cb[:mw, :ntok], in0=acc[:mw, :ntok], in1=gb[:mw, :ntok],
                                                                       op=ALU.mult), [r_acc, r_gb], [r_cb])
                                tr.op("dve", lambda e: e.tensor_tensor(out=mT[:mw, m, :ntok], in0=cb[:mw, :ntok], in1=mT[:mw, m, :ntok],
                                                                       op=ALU.add), [r_cb], [r_big])
                        gemm_fm(hnT, r_hnT, nfe // 128, ntok, Wb, 0, D, pacc, r_pacc, ev_y)
                    for ti, (t0, rows) in enumerate(tiles):
                        tr.dma("sp", htile[:rows], grp[ti][0], writes=[r_h], lane_res=r_h)
                        gemm_tm(mT, r_big, KC, [(t0, rows)], I["w_out"], 0, D, [pacc[ti % 4]], [r_pacc[ti % 4]],
                                lambda _t, b0, w, acc, r_acc, rows=rows: tr.op("dve", lambda e: e.tensor_tensor(
                                    out=htile[:rows, b0:b0 + w], in0=acc[:rows, :w], in1=htile[:rows, b0:b0 + w], op=ALU.add),
                                    [r_acc], [r_h]))
                        tr.dma("sp", Hs[col0 + t0:col0 + t0 + rows, :], htile[:rows], reads=[r_h], writes=[r_Hs], lane_res=r_h)
                        rstd_of(htile[:rows, :], rows, D, xs_bf, r_xs, ss, rstd, r_h, r_st)
                        tr.op("dve", lambda e, rows=rows: e.scalar_tensor_tensor(
                            out=xs_bf[:rows, :D], in0=htile[:rows], scalar=rstd[:rows, 0:1], in1=gbc[:rows],
                            op0=ALU.mult, op1=ALU.mult), [r_h, r_st, r_gbc], [r_xs])
                        p, rp = nxt_pT()
                        transposes(xs_bf, rows, KC, p, rp,
                                   lambda j0, n, t0=t0, rows=rows: hnT[:, j0:j0 + n, t0:t0 + rows], r_xs, r_hnT)
                    for f in range(NF):
                        if True:
                            wg, r_wg = load_half(I["w_fg"], f)
                            wu, r_wu = load_half(I["w_fu"], f)
                            ag, r_ag = pacc[(f % 2)], r_pacc[(f % 2)]
                            au, r_au = pacc[2 + (f % 2)], r_pacc[2 + (f % 2)]
                            for (wt, r_w, acc, r_acc) in ((wg, r_wg, ag, r_ag), (wu, r_wu, au, r_au)):
                                def fmm(e, wt=wt, acc=acc):
                                    ins = None
                                    for k in range(KC):
                                        ins = e.matmul(acc[:, :ntok], lhsT=wt[:, k, :], rhs=hnT[:, k, :ntok],
                                                       start=(k == 0), stop=(k == KC - 1))
                                    return ins
                                tr.op("pe", fmm, [r_hnT, r_w], [r_acc])
                            if npr > 0:
                                tr.op("act", lambda e, ag=ag: e.activation(out=ugP[:, 2:2 + npr], in_=ag[:, :npr], func=AF.Copy),
                                      [r_ag], [r_ug])
                                tr.op("dve", lambda e, f=f: e.tensor_copy(out=ugP[:, 0:2], in_=carry[:, f, :]), [r_carry], [r_ug])
                                if gi == 0:
                                    tr.op("dve", lambda e: e.tensor_scalar(out=ugP[:, 128:130], in0=ugP[:, 128:130], scalar1=flag[:, 0:1],
                                                                           scalar2=None, op0=ALU.mult), [r_const], [r_ug])
                                tr.op("dve", lambda e, f=f: e.tensor_copy(out=carry[:, f, :], in_=ugP[:, npr:npr + 2]), [r_ug], [r_carry])
                                if last:
                                    tr.op("dve", lambda e, f=f: e.tensor_copy(out=cv[:, 0:2, f], in_=ugP[:, npr:npr + 2]), [r_ug], [r_cv])
                            if has_s:
                                tr.op("act", lambda e, ag=ag: e.activation(out=ugS[:, 2:66], in_=ag[:, npr:npr + 64], func=AF.Copy),
                                      [r_ag], [r_ug])
                                tr.op("dve", lambda e, f=f: e.tensor_copy(out=ugS[:, 0:2], in_=stc[:, 2 * f:2 * f + 2]), [r_const], [r_ug])
                                tr.op("dve", lambda e, f=f: e.tensor_copy(out=cv[:, 2:4, f], in_=ugS[:, 64:66]), [r_ug], [r_cv])
                            for (ug, c0_, n_) in ((ugP, 0, npr), (ugS, npr, 64 if has_s else 0)):
                                if n_ == 0:
                                    continue
                                tr.op("act", lambda e, ug=ug, c0_=c0_, n_=n_, f=f: e.activation(
                                    out=cb[:, c0_:c0_ + n_], in_=ug[:, 2:2 + n_], func=AF.Identity,
                                    bias=bcv[:, f:f + 1], scale=wcv[:, 2 * NF + f:2 * NF + f + 1]), [r_ug, r_const], [r_cb])
                                for j in (1, 0):
                                    tr.op("dve", lambda e, ug=ug, c0_=c0_, n_=n_, f=f, j=j: e.scalar_tensor_tensor(
                                        out=cb[:, c0_:c0_ + n_], in0=ug[:, j:j + n_], scalar=wcv[:, j * NF + f:j * NF + f + 1],
                                        in1=cb[:, c0_:c0_ + n_], op0=ALU.mult, op1=ALU.add), [r_ug, r_const], [r_cb])
                            tr.op("act", lambda e: e.activation(out=cb[:, :ntok], in_=cb[:, :ntok], func=AF.Silu), [r_cb], [r_cb])
                            tr.op("dve", lambda e, f=f, au=au: e.tensor_tensor(out=actT[:, f, :ntok], in0=cb[:, :ntok], in1=au[:, :ntok],
                                                                             op=ALU.mult), [r_cb, r_au], [r_big])
                    def ev_d(ti, b0, w, acc, r_acc, tiles=tiles, col0=col0):
                        t0, rows = tiles[ti]
                        tr.dma("sp", hblk[:rows, :w], Hs[col0 + t0:col0 + t0 + rows, b0:b0 + w], reads=[r_Hs], writes=[r_hblk], lane_res=r_hblk)
                        tr.op("dve", lambda e: e.tensor_tensor(out=yblk[:rows, :w], in0=acc[:rows, :w], in1=hblk[:rows, :w], op=ALU.add),
                              [r_acc, r_hblk], [r_yblk])
                        tr.dma("sp", Ys[col0 + t0:col0 + t0 + rows, b0:b0 + w], yblk[:rows, :w], reads=[r_yblk], lane_res=r_yblk)
                    gemm_tm(actT, r_big, NF, tiles, I["w_fd"], 0, D, pacc, r_pacc, ev_d)
                for gi, grp in enumerate(QGROUPS):
                    do_group4(gi, grp)
                for j, dst in ((0, O["conv_p"][0]), (1, O["conv_p"][1]), (2, O["conv_s"][0]), (3, O["conv_s"][1])):
                    if cfg.get("p4cut", 9) < 5:
                        break
                    dv_ = dst.rearrange("(f p) -> p f", p=128)
                    for f0 in range(0, NF, 8):
                        f1 = min(NF, f0 + 8)
                        ln = tr.lane_of(r_cv)
                        tr.op("sp", lambda e, o=dv_[:, f0:f1], i=cv[:, j, f0:f1]: e.dma_start(
                            out=o, in_=i, allow_slow_non_contiguous=True), [r_cv], [], lane=ln)
                tr.barrier()
                tr.dma("sp", gbc[:], I["g_final"][:], writes=[r_gbc], lane_res=r_gbc)
                for j, (src, rows, _, col) in enumerate(QTILES):
                    if j == 0:
                        continue
                    dst = O["y_s"][:, :] if rows == 64 else O["y_p"][(j - 1) * 128:j * 128, :]
                    tr.dma("sp", htile[:rows], Ys[col:col + rows, :], writes=[r_h], lane_res=r_h)
                    rstd_of(htile[:rows, :], rows, D, xs_bf, r_xs, ss, rstd, r_h, r_st)
                    tr.op("dve", lambda e, rows=rows: e.scalar_tensor_tensor(
                        out=htile[:rows], in0=htile[:rows], scalar=rstd[:rows, 0:1], in1=gbc[:rows],
                        op0=ALU.mult, op1=ALU.mult), [r_st, r_gbc], [r_h])
                    tr.dma("sp", dst, htile[:rows], reads=[r_h], lane_res=r_h)

        phase4()
        tr.barrier()
        with nc.Block() as block:
            tr.emit(block)
    return nc


def build_rest(L):
    raise NotImplementedError


def rope_tables(pos, half):
    inv = np.power(np.float32(ROPE_THETA), -np.arange(half, dtype=np.float32) / np.float32(half)).astype(np.float32)
    ang = pos.astype(np.float32)[:, None] * inv[None, :]
    return np.cos(ang).astype(np.float32), np.sin(ang).astype(np.float32)


def make_in_maps(cfg, inp):
    D, SEQ, PAST = cfg["D"], cfg["SEQ"], cfg["PAST"]
    HA, HB, QL, KVL, FF = cfg["HA"], cfg["HB"], cfg["QL"], cfg["KVL"], cfg["FF"]
    TB = SEQ // 8
    NQT = TB // 128 + 1
    NKT = SEQ // 128
    KC = D // 128
    NF = FF // 128
    f = lambda a: np.ascontiguousarray(np.asarray(a, dtype=np.float32))
    bc = lambda v: np.ascontiguousarray(np.broadcast_to(f(v).reshape(1, -1), (128, f(v).size)))
    xp = f(inp["x_prompt"])[0]
    w_kvb = f(inp["w_kvb"])[0].reshape(KVL, HB, 256)
    shared = dict(
        w_in=f(inp["w_in"])[0], w_qb=f(inp["w_qb"])[0],
        w_kvn=np.ascontiguousarray(w_kvb[:, :, :128].reshape(KVL, HB * 128)),
        w_kvv=np.ascontiguousarray(w_kvb[:, :, 128:].reshape(KVL, HB * 128)),
        w_ba=f(inp["w_branch_a"])[0], w_bb=f(inp["w_branch_b"])[0], w_out=f(inp["w_out"])[0],
        w_fg=f(inp["w_ff_gate"])[0], w_fu=f(inp["w_ff_up"])[0], w_fd=f(inp["w_ff_down"])[0],
        g_attn=bc(inp["g_attn"][0]), g_ffn=bc(inp["g_ffn"][0]), g_final=bc(inp["g_final"]),
        g_qa=bc(inp["g_qa"][0]), g_kva=bc(inp["g_kva"][0]), g_sub=bc(inp["g_subln"][0]),
        b_gate=np.ascontiguousarray(f(inp["b_gate"])[0].reshape(2 * KC, 128).T),
        w_conv=np.ascontiguousarray(f(inp["w_conv"])[0].reshape(3, NF, 128).transpose(2, 0, 1).reshape(128, 3 * NF)),
        b_conv=np.ascontiguousarray(f(inp["b_conv"])[0].reshape(NF, 128).T),
        lam=np.ascontiguousarray(np.broadcast_to(np.concatenate(
            [f(inp[k])[0] for k in ("lam_q1", "lam_k1", "lam_q2", "lam_k2")]).reshape(1, 256), (128, 256))),
        ident=np.eye(128, dtype=np.float32).astype(ml_dtypes.bfloat16), identf=np.eye(128, dtype=np.float32),
    )
    maps = []
    for c in range(8):
        g0 = TB * c - 128
        pos = (g0 + np.arange(SEQ)) % SEQ
        pos_all = np.concatenate([pos, PAST + np.arange(64)])
        cA, sA = rope_tables(pos_all, 8)
        cB, sB = rope_tables(pos_all, 32)
        visb = np.zeros((128, NKT), np.float32)
        for i in range(NKT):
            if i == 0:
                vis = c > 0
            elif i < NQT:
                vis = True
            else:
                vis = (g0 + 128 * i) >= SEQ
            visb[:, i] = 0.0 if vis else NEGB
        m = dict(shared)
        m.update(
            xk=np.ascontiguousarray(xp[pos]), xs=f(inp["x_sample"])[c],
            c_dk=f(inp["cache_dk"])[0, c].reshape(PAST, -1), c_dv=f(inp["cache_dv"])[0, c].reshape(PAST, -1),
            c_ckv=f(inp["cache_ckv"])[0, c], c_kpe=f(inp["cache_kpe"])[0, c],
            st_conv=np.ascontiguousarray(f(inp["state_conv"])[0, c].reshape(2, NF, 128).transpose(2, 1, 0).reshape(128, 2 * NF)),
            cosA=np.ascontiguousarray(np.tile(cA, (1, 2 * HA))), sinA=np.ascontiguousarray(np.tile(sA, (1, 2 * HA))),
            cosB=np.ascontiguousarray(np.tile(cB, (1, HB))), sinB=np.ascontiguousarray(np.tile(sB, (1, HB))),
            visb=visb, flag=np.full((128, 1), 0.0 if c == 0 else 1.0, np.float32),
        )
        maps.append(m)
    return maps


def assemble(cfg, res):
    D, SEQ = cfg["D"], cfg["SEQ"]
    HA, HB, KVL, FF = cfg["HA"], cfg["HB"], cfg["KVL"], cfg["FF"]
    R = res
    cat = lambda k: np.concatenate([R[c][k] for c in range(8)], axis=0)
    stk = lambda k: np.stack([R[c][k] for c in range(8)], axis=0)
    return (
        cat("y_p")[None], stk("y_s"),
        cat("dk_p").reshape(1, 1, SEQ, 2 * HA, 64), cat("dv_p").reshape(1, 1, SEQ, HA, 128),
        cat("ckv_p").reshape(1, 1, SEQ, KVL), cat("kpe_p").reshape(1, 1, SEQ, 64),
        R[7]["conv_p"].reshape(1, 1, 2, FF),
        stk("dk_s").reshape(1, 8, 64, 2 * HA, 64), stk("dv_s").reshape(1, 8, 64, HA, 128),
        stk("ckv_s").reshape(1, 8, 64, KVL), stk("kpe_s").reshape(1, 8, 64, 64),
        stk("conv_s").reshape(1, 8, 2, FF),
    )


def run(cfg, inp):
    nc = build(cfg)
    maps = make_in_maps(cfg, inp)
    res = run_bass_kernel_spmd(nc, maps, core_ids=list(range(8)))
    return assemble(cfg, res.results)


def kernel(**inputs):
    outs = run(FULL, inputs)
    return tuple(np.ascontiguousarray(o, dtype=np.float32) for o in outs)
```
